# Optimizing a Trainium2 kernel written in Bass

```python
import jax, jax.numpy as jnp
from jax import lax
import numpy as np

D_MODEL = 2048
BATCH = 1
SEQ = 8192
DEPTH = 1

MEM_LEN = 256
HEAD_DIM = 128
N_CONV_GROUPS = 4
CONV_W = N_CONV_GROUPS * HEAD_DIM
CONV_K = 31
N_FOX_HEADS = 8
FOX_W = N_FOX_HEADS * HEAD_DIM
N_MEM_HEADS = 4
MEM_W = N_MEM_HEADS * HEAD_DIM
MIX_W = CONV_W + FOX_W + MEM_W
Q_BLOCK = 128
EPS = 1e-6
FORGET_BIAS_INIT = 2.0

SPLITS = (
    CONV_W,
    2 * CONV_W,
    3 * CONV_W,
    3 * CONV_W + FOX_W,
    3 * CONV_W + 2 * FOX_W,
    3 * CONV_W + 3 * FOX_W,
    3 * CONV_W + 3 * FOX_W + N_FOX_HEADS,
    3 * CONV_W + 4 * FOX_W + N_FOX_HEADS,
    3 * CONV_W + 4 * FOX_W + N_FOX_HEADS + MEM_W,
)
D_IN = 3 * CONV_W + 4 * FOX_W + N_FOX_HEADS + 2 * MEM_W

kernel_name = "hybrid_conformer_fox_memory_layer"


def rmsnorm(x, g):
    xf = x.astype(jnp.float32)
    y = xf * lax.rsqrt(jnp.mean(xf * xf, axis=-1, keepdims=True) + EPS)
    return (y * g.astype(jnp.float32)).astype(x.dtype)


def layernorm(x, g, b):
    xf = x.astype(jnp.float32)
    mu = jnp.mean(xf, axis=-1, keepdims=True)
    var = jnp.mean(jnp.square(xf - mu), axis=-1, keepdims=True)
    y = (xf - mu) * lax.rsqrt(var + EPS)
    return (y * g.astype(jnp.float32) + b.astype(jnp.float32)).astype(x.dtype)


def causal_depthwise_conv(u, w, b):
    y = lax.conv_general_dilated(
        u, w[:, None, :], window_strides=(1,), padding=[(CONV_K - 1, 0)],
        dimension_numbers=('NWC', 'WIO', 'NWC'), feature_group_count=u.shape[-1])
    return y + b


def fox_attention(q, k, v, logf):
    B, S, H, Dh = q.shape
    nb = S // Q_BLOCK
    cT = jnp.cumsum(logf, axis=1).transpose(0, 2, 1)
    qb = q.reshape(B, nb, Q_BLOCK, H, Dh).transpose(1, 0, 2, 3, 4)
    cb = cT.reshape(B, H, nb, Q_BLOCK).transpose(2, 0, 1, 3)
    k_pos = jnp.arange(S)
    scale = Dh ** -0.5

    def block(args):
        i, qi, ci = args
        s = jnp.einsum('bqhd,bkhd->bhqk', qi, k).astype(jnp.float32) * scale
        bias = ci[..., :, None] - cT[..., None, :]
        q_pos = i * Q_BLOCK + jnp.arange(Q_BLOCK)
        mask = k_pos[None, :] <= q_pos[:, None]
        s = jnp.where(mask, s + bias, -jnp.inf)
        p = jax.nn.softmax(s, axis=-1).astype(v.dtype)
        return jnp.einsum('bhqk,bkhd->bqhd', p, v)

    out = lax.map(block, (jnp.arange(nb), qb, cb))
    return out.transpose(1, 0, 2, 3, 4).reshape(B, S, H * Dh)


def memory_attention(q, mk, mv):
    B, S, H, Dh = q.shape
    s = jnp.einsum('bshd,bmhd->bhsm', q, mk).astype(jnp.float32) * (Dh ** -0.5)
    p = jax.nn.softmax(s, axis=-1).astype(mv.dtype)
    return jnp.einsum('bhsm,bmhd->bshd', p, mv).reshape(B, S, H * Dh)


def setup_inputs(seed: int = 0) -> dict:
    key = jax.random.key(seed)
    ks = jax.random.split(key, 16)
    n = jax.random.normal
    f32 = jnp.float32
    return {
        "x": n(ks[0], (BATCH, SEQ, D_MODEL), f32),
        "mem": n(ks[1], (BATCH, MEM_LEN, D_MODEL), f32),
        "norm_g": 1.0 + 0.05 * n(ks[2], (DEPTH, D_MODEL), f32),
        "mem_norm_g": 1.0 + 0.05 * n(ks[3], (DEPTH, D_MODEL), f32),
        "w_in": n(ks[4], (DEPTH, D_MODEL, D_IN), f32) * D_MODEL ** -0.5,
        "b_f": FORGET_BIAS_INIT + 0.1 * n(ks[5], (DEPTH, N_FOX_HEADS), f32),
        "conv_w": n(ks[6], (DEPTH, CONV_K, CONV_W), f32) * CONV_K ** -0.5,
        "conv_b": 0.02 * n(ks[7], (DEPTH, CONV_W), f32),
        "conv_ln_g": 1.0 + 0.05 * n(ks[8], (DEPTH, CONV_W), f32),
        "conv_ln_b": 0.02 * n(ks[9], (DEPTH, CONV_W), f32),
        "w_conv_pw": n(ks[10], (DEPTH, CONV_W, CONV_W), f32) * CONV_W ** -0.5,
        "w_mem_kv": n(ks[11], (DEPTH, D_MODEL, 2 * MEM_W), f32) * D_MODEL ** -0.5,
        "w_out": n(ks[12], (DEPTH, MIX_W, D_MODEL), f32) * MIX_W ** -0.5,
        "final_g": 1.0 + 0.05 * n(ks[13], (D_MODEL,), f32),
    }


def reference(x, mem, norm_g, mem_norm_g, w_in, b_f, conv_w, conv_b, conv_ln_g,
              conv_ln_b, w_conv_pw, w_mem_kv, w_out, final_g):
    B, S, _ = x.shape
    for l in range(DEPTH):
        h = rmsnorm(x, norm_g[l])
        proj = h @ w_in[l]
        (cv_a, cv_b, cv_gate, fq, fk, fv, f_logit, fox_gate,
         mq, mem_gate) = jnp.split(proj, SPLITS, axis=-1)

        u = cv_a * jax.nn.sigmoid(cv_b)
        u = causal_depthwise_conv(u, conv_w[l], conv_b[l])
        u = jax.nn.silu(layernorm(u, conv_ln_g[l], conv_ln_b[l]))
        y_conv = (u @ w_conv_pw[l]) * jax.nn.silu(cv_gate)

        logf = jax.nn.log_sigmoid(f_logit.astype(jnp.float32) + b_f[l].astype(jnp.float32))
        shp = (B, S, N_FOX_HEADS, HEAD_DIM)
        y_fox = fox_attention(fq.reshape(shp), fk.reshape(shp), fv.reshape(shp), logf)
        y_fox = y_fox * jax.nn.silu(fox_gate)

        mkv = rmsnorm(mem, mem_norm_g[l]) @ w_mem_kv[l]
        mk, mv = jnp.split(mkv, 2, axis=-1)
        mshp = (B, mem.shape[1], N_MEM_HEADS, HEAD_DIM)
        y_mem = memory_attention(mq.reshape(B, S, N_MEM_HEADS, HEAD_DIM),
                                 mk.reshape(mshp), mv.reshape(mshp))
        y_mem = y_mem * jax.nn.silu(mem_gate)

        y = jnp.concatenate([y_conv, y_fox, y_mem], axis=-1)
        x = x + y @ w_out[l]
    return rmsnorm(x, final_g)
```

```python
from contextlib import ExitStack

import numpy as np
import ml_dtypes
import concourse.bass as bass
import concourse.mybir as mybir
from concourse.bass_utils import run_bass_kernel_spmd

F32 = mybir.dt.float32
BF16 = mybir.dt.bfloat16
AF = mybir.ActivationFunctionType
ALU = mybir.AluOpType
AX = mybir.AxisListType

NCORES = 8
S = 8192
D = 2048
NT = S // 128
NJ = 8
H = 8
DIN = 6664
EPS = 1e-6
SCALE = 128.0 ** -0.5
C_A, C_B, C_G = 0, 512, 1024
C_Q, C_K, C_V, C_F, C_FG, C_MQ, C_MG = 1536, 2560, 3584, 4608, 4616, 5640, 6152
NEG = -1.0e30
STAGE = 99
ACT_ONLY_OF_9 = 5
DEBUG = None


class R:
    __slots__ = ("w", "rs")

    def __init__(self):
        self.w = None
        self.rs = {}


class EngState:
    def __init__(self, sched, name):
        self.sched = sched
        self.name = name
        self.prog = []
        self.sem = sched.new_sem(name + "_s0")
        self.nsem = 1
        self.cnt = 0
        self.seen = {}

    def wait(self, toks):
        best = {}
        for t in toks:
            if t is None:
                continue
            sem, val = t
            if sem is self.sem and val > self.cnt:
                continue
            k = id(sem)
            if self.seen.get(k, 0) >= val:
                continue
            if k not in best or best[k][1] < val:
                best[k] = (sem, val)
        for k, (sem, val) in best.items():
            self.seen[k] = val
            self.prog.append(("wait", sem, val))

    def emit(self, fn, sig, inc=1, sem=None):
        if sem is None:
            if sig:
                if self.cnt >= 16000:
                    self.sem = self.sched.new_sem("%s_s%d" % (self.name, self.nsem))
                    self.nsem += 1
                    self.cnt = 0
                self.cnt += inc
                self.prog.append(("op", fn, self.sem, inc))
                return (self.sem, self.cnt)
            self.prog.append(("op", fn, None, 0))
            return (self.sem, self.cnt + 1)
        self.prog.append(("op", fn, sem, inc))
        return None


class Sched:
    def __init__(self, nc, es):
        self.nc = nc
        self.es = es
        self.nsems = 0
        self.E = {n: EngState(self, n) for n in ("sync", "scalar", "vector", "gpsimd", "tensor")}
        self.dsem = {}

    def new_sem(self, name):
        self.nsems += 1
        return self.es.enter_context(self.nc.semaphore(name))

    def op(self, ename, fn, reads=(), writes=(), sig=True, after=()):
        e = self.E[ename]
        deps = list(after)
        for r in reads:
            deps.append(r.w)
        for w in writes:
            deps.append(w.w)
            deps.extend(w.rs.values())
        e.wait(deps)
        tok = e.emit(fn, sig)
        self._upd(tok, reads, writes)
        return tok

    def _upd(self, tok, reads, writes):
        k = id(tok[0])
        for r in reads:
            if k not in r.rs or r.rs[k][1] < tok[1]:
                r.rs[k] = tok
        for w in writes:
            w.w = tok
            w.rs = {}

    def dma(self, qname, key, out, in_, reads=(), writes=(), after=()):
        e = self.E[qname]
        deps = list(after)
        for r in reads:
            deps.append(r.w)
        for w in writes:
            deps.append(w.w)
            deps.extend(w.rs.values())
        e.wait(deps)
        if key not in self.dsem:
            self.dsem[key] = [self.new_sem("d_" + key), 0]
        ds = self.dsem[key]
        ds[1] += 16
        e.emit(lambda eng: eng.dma_start(out=out, in_=in_), True, 16, sem=ds[0])
        tok = (ds[0], ds[1])
        self._upd(tok, reads, writes)
        return tok

    def replay(self, ename, eng):
        for it in self.E[ename].prog:
            if it[0] == "wait":
                eng.wait_ge(it[1], it[2])
            else:
                inst = it[1](eng)
                if it[2] is not None:
                    inst.then_inc(it[2], it[3])

    def final_tokens(self):
        toks = []
        for e in self.E.values():
            if e.cnt > 0:
                toks.append((e.sem, e.cnt))
        for ds in self.dsem.values():
            toks.append((ds[0], ds[1]))
        return toks


def build_nc(stage=STAGE, debug=DEBUG):
    nc = bass.Bass("TRN2", target_bir_lowering=False)

    def din(name, shape, dt=F32):
        return nc.dram_tensor(name, list(shape), dt, kind="ExternalInput").ap()

    x_all = din("x_all", [S, D])
    x_own = din("x_own", [NJ * 128, D])
    x_halo = din("x_halo", [256, D])
    mem = din("mem", [256, D])
    g_rep_d = din("norm_g_rep", [128, D])
    mg_rep_d = din("mem_norm_g_rep", [128, D])
    fg_rep_d = din("final_g_rep", [128, D])
    w_in = din("w_in", [D, DIN])
    bf_rep_d = din("b_f_rep", [128, H])
    convw_d = din("conv_w_t", [128, 4, 31])
    convp_d = din("conv_par", [128, 3, 4])
    w_pw = din("w_conv_pw", [512, 512])
    w_mkv = din("w_mem_kv", [D, 1024])
    w_out = din("w_out", [D, D])
    ident_d = din("ident", [128, 128], BF16)
    uinc_d = din("uinc", [128, 128])
    maskadd_d = din("maskadd", [128, 8, 128], BF16)
    sel_d = din("sel", [128, NJ, NT])
    out = nc.dram_tensor("out", [NJ * 128, D], F32, kind="ExternalOutput").ap()
    kscr = nc.dram_tensor("kscr", [H, 128, S], BF16, kind="Internal").ap()
    vscr = nc.dram_tensor("vscr", [H, 128, NT, 129], BF16, kind="Internal").ap()
    hscr = nc.dram_tensor("hscr", [128, 16, 1280], BF16, kind="Internal").ap()
    wobf = nc.dram_tensor("wobf", [D, D], BF16, kind="Internal").ap()
    r_wobf = [R() for _ in range(4)]
    hmscr = nc.dram_tensor("hmscr", [128, 16, 256], BF16, kind="Internal").ap()
    qscr = nc.dram_tensor("qscr", [H, 128, 1024], BF16, kind="Internal").ap()
    gscr = nc.dram_tensor("gscr", [H, 128, NJ, 128], BF16, kind="Internal").ap()
    dbg = None
    if debug is not None:
        dbg = nc.dram_tensor("dbg", list(debug[1]), debug[2], kind="ExternalOutput").ap()

    es = ExitStack()
    with es:
        sc = Sched(nc, es)
        cnt = [0]

        def sb(scope, shape, dt, name=None):
            cnt[0] += 1
            return scope.enter_context(nc.sbuf_tensor("%s_%d" % (name or "t", cnt[0]), list(shape), dt))

        def ps(scope, shape, dt, name=None):
            cnt[0] += 1
            return scope.enter_context(nc.psum_tensor("%s_%d" % (name or "p", cnt[0]), list(shape), dt))

        V, A, P, G, SY = "vector", "scalar", "tensor", "gpsimd", "sync"

        def wload(dst, rdst, key, c0, ncols, src=None, rows=D):
            src = w_in if src is None else src
            kt = rows // 128
            half = max(1, kt // 2)
            for a in range(0, kt, half):
                sc.dma(G, key, dst[:, a:a + half, 0:ncols],
                       src[a * 128:(a + half) * 128, c0:c0 + ncols].rearrange("(k p) c -> p k c", p=128),
                       writes=[rdst])

        top = es
        ident = sb(top, [128, 128], BF16, "ident"); r_ident = R()
        uinc = sb(top, [128, 128], F32, "uinc"); r_uinc = R()
        ones_f = sb(top, [128, 128], F32, "onesf"); r_onesf = R()
        g_rep = sb(top, [128, D], F32, "grep"); r_grep = R()
        bf_rep = sb(top, [128, H], F32, "bfrep"); r_bf = R()
        zcol = sb(top, [128, NT, H], F32, "zcol"); r_zcol = R()
        sc.dma(SY, "c0a", ident[:], ident_d, writes=[r_ident])
        sc.dma(SY, "c0b", uinc[:], uinc_d, writes=[r_uinc])
        sc.dma(SY, "c0c", g_rep[:], g_rep_d, writes=[r_grep])
        sc.dma(SY, "c0d", bf_rep[:], bf_rep_d, writes=[r_bf])
        sc.op(V, lambda e: e.memset(ones_f[:], 1.0), writes=[r_onesf])
        cst = sb(top, [128, 4], F32, "cst"); r_cst = R()
        sc.op(V, lambda e: e.memset(cst[:, 0:1], EPS), writes=[r_cst])
        sc.op(V, lambda e: e.memset(cst[:, 1:2], 1.0), writes=[r_cst])

        def rmsnorm_tile(scope_bufs, src_ap, grep, r_g, hdst, r_hdst, key):
            xb, r_xb, ss, r_ss = scope_bufs
            sc.dma(SY, key, xb[:], src_ap, writes=[r_xb])
            sc.op(A, lambda e: e.activation(out=hdst, in_=xb[:], func=AF.Square, accum_out=ss[:, 0:1]),
                  reads=[r_xb], writes=[r_hdst, r_ss])
            sc.op(A, lambda e: e.activation(out=ss[:, 1:2], in_=ss[:, 0:1], func=AF.Sqrt, bias=cst[:, 0:1], scale=1.0 / D),
                  reads=[r_ss, r_cst], writes=[r_ss])
            sc.op(V, lambda e: e.reciprocal(out=ss[:, 2:3], in_=ss[:, 1:2]), reads=[r_ss], writes=[r_ss])
            sc.op(V, lambda e: e.scalar_tensor_tensor(out=hdst, in0=xb[:], scalar=ss[:, 2:3], in1=grep[:],
                                                      op0=ALU.mult, op1=ALU.mult),
                  reads=[r_xb, r_ss, r_g], writes=[r_hdst])

        def transpose_tile(h_ap, r_h, pst, r_pst, dst_fn, r_dst, evac_eng):
            for half in range(2):
                pt, r_pt = pst[half], r_pst[half]
                for kk in range(8):
                    k = half * 8 + kk
                    sc.op(P, lambda e, k=k, kk=kk, pt=pt: e.transpose(out=pt[:, kk, :], in_=h_ap[:, k * 128:(k + 1) * 128],
                                                                       identity=ident[:]),
                          reads=[r_h, r_ident], writes=[r_pt], sig=(kk == 7))
                dst = dst_fn(half * 8, half * 8 + 8)
                if evac_eng == A:
                    sc.op(A, lambda e, pt=pt, dst=dst: e.copy(out=dst, in_=pt[:]), reads=[r_pt], writes=[r_dst])
                else:
                    sc.op(V, lambda e, pt=pt, dst=dst: e.tensor_copy(out=dst, in_=pt[:]), reads=[r_pt], writes=[r_dst])

        r_kchunk = [R() for _ in range(16)]
        r_vchunk = [R() for _ in range(16)]
        with ExitStack() as p1:
            wk = sb(p1, [128, 16, 1024], BF16, "wk"); r_wk = R()
            wv = sb(p1, [128, 16, 1032], BF16, "wv"); r_wv = R()
            r_wkh = [R() for _ in range(H)]
            r_wvg = [R() for _ in range(3)]
            NXB = 5
            xbs = [(sb(p1, [128, D], F32, "xb"), R(), sb(p1, [128, 4], F32, "ss"), R()) for _ in range(NXB)]
            hbs = [(sb(p1, [128, D], BF16, "hb"), R()) for _ in range(4)]
            hTs = [(sb(p1, [128, 16, 512], BF16, "hT"), R()) for _ in range(2)]
            ksts = [(sb(p1, [128, H, 512], BF16, "kst"), R()) for _ in range(1)]
            vsts = [(sb(p1, [128, H, 4, 129], BF16, "vst"), R()) for _ in range(2)]
            for vv in range(2):
                sc.op(G, lambda e, vv=vv: e.memset(vsts[vv][0][:], 1.0), writes=[vsts[vv][1]])
            pT = [ps(p1, [128, 8, 128], BF16, "pT") for _ in range(2)]; r_pT = [R(), R()]
            pmm = [ps(p1, [128, 512], F32, "pmm") for _ in range(4)]; r_pmm = [R() for _ in range(4)]
            pz = ps(p1, [128, 512], F32, "pz"); r_pz = R()
            nchunks = 16 if stage >= 2 else 1
            cnt1 = {"mi": 0}

            tcn = {"t": 0}
            slot_of = {}
            r_hscr = [R() for _ in range(10)]
            hsos = [(sb(p1, [128, 16, 128], BF16, "hso"), R()) for _ in range(2)]

            def rms(g, i):
                t = tcn["t"]
                tcn["t"] += 1
                slot_of[(g, i)] = t
                ti = g * 4 + i
                hb, r_hb = hbs[t % 4]
                rmsnorm_tile(xbs[t % NXB], x_all[ti * 128:(ti + 1) * 128, :], g_rep, r_grep, hb[:], r_hb, "xb%d" % (t % NXB))

            def tr(g, i):
                t = slot_of[(g, i)]
                hb, r_hb = hbs[t % 4]
                hT, r_hT = hTs[g % 2]
                transpose_tile(hb, r_hb, pT, r_pT, lambda k0, k1, i=i, hT=hT: hT[:, k0:k1, i * 128:(i + 1) * 128],
                               r_hT, A if (t % 2 == 0) else V)

            def rms_own(ot):
                t = tcn["t"]
                tcn["t"] += 1
                slot_of[("own", ot)] = t
                hb, r_hb = hbs[t % 4]
                src = x_own[ot * 128:(ot + 1) * 128, :] if ot < 8 else x_halo[(ot - 8) * 128:(ot - 7) * 128, :]
                rmsnorm_tile(xbs[t % NXB], src, g_rep, r_grep, hb[:], r_hb, "xb%d" % (t % NXB))

            mg_rep1 = sb(p1, [128, D], F32, "mgrep1"); r_mg1 = R()
            sc.dma(SY, "c3", mg_rep1[:], mg_rep_d, writes=[r_mg1])
            r_hmscr = [R(), R()]

            def rms_mem(mt):
                t = tcn["t"]
                tcn["t"] += 1
                slot_of[("mem", mt)] = t
                hb, r_hb = hbs[t % 4]
                rmsnorm_tile(xbs[t % NXB], mem[mt * 128:(mt + 1) * 128, :], mg_rep1, r_mg1, hb[:], r_hb, "xb%d" % (t % NXB))

            def tr_mem(mt):
                t = slot_of[("mem", mt)]
                hb, r_hb = hbs[t % 4]
                hso, r_hso = hsos[mt % 2]
                transpose_tile(hb, r_hb, pT, r_pT, lambda k0, k1, hso=hso: hso[:, k0:k1, :], r_hso, A if (t % 2 == 0) else V)
                sc.dma(SY, "hso%d" % (mt % 2), hmscr[:, :, mt * 128:(mt + 1) * 128], hso[:], reads=[r_hso], writes=[r_hmscr[mt]])

            def tr_own(ot):
                t = slot_of[("own", ot)]
                hb, r_hb = hbs[t % 4]
                hso, r_hso = hsos[ot % 2]
                transpose_tile(hb, r_hb, pT, r_pT, lambda k0, k1, hso=hso: hso[:, k0:k1, :], r_hso, A if (t % 2 == 0) else V)
                sc.dma(SY, "hso%d" % (ot % 2), hscr[:, :, ot * 128:(ot + 1) * 128], hso[:], reads=[r_hso], writes=[r_hscr[ot]])

            for i in range(4):
                rms(0, i)
            first_x = [xbs[i][1].w for i in range(4)]
            for hh in range(H):
                sc.dma(G, "wk%d" % hh, wk[:, :, hh * 128:(hh + 1) * 128],
                       w_in[:, C_K + hh * 128:C_K + (hh + 1) * 128].rearrange("(k p) c -> p k c", p=128),
                       after=(first_x if hh >= 1 else []), writes=[r_wkh[hh]])
            for cg in range(2):
                sc.dma(G, "wv%d" % cg, wv[:, :, cg * 512:(cg + 1) * 512],
                       w_in[:, C_V + cg * 512:C_V + (cg + 1) * 512].rearrange("(k p) c -> p k c", p=128), after=first_x, writes=[r_wvg[cg]])
            sc.dma(G, "wv2", wv[:, :, 1024:1032], w_in[:, C_F:C_F + 8].rearrange("(k p) c -> p k c", p=128), after=first_x, writes=[r_wvg[2]])
            for i in range(4):
                tr(0, i)
            for g in range(nchunks):
                hT, r_hT = hTs[g % 2]
                kst, r_kst = ksts[0]
                vst, r_vst = vsts[g % 2]
                for hh in range(H):
                    mi = cnt1["mi"]
                    pm, r_pm = pmm[mi % 4], r_pmm[mi % 4]
                    for k in range(16):
                        sc.op(P, lambda e, k=k, hh=hh, pm=pm, hT=hT: e.matmul(out=pm[:], lhsT=wk[:, k, hh * 128:(hh + 1) * 128],
                                                                               rhs=hT[:, k, :], start=(k == 0), stop=(k == 15)),
                              reads=[r_wkh[hh], r_hT], writes=[r_pm], sig=(k == 15))
                    if mi % 2 == 0:
                        sc.op(A, lambda e, pm=pm, kst=kst, hh=hh: e.copy(out=kst[:, hh, :], in_=pm[:]), reads=[r_pm], writes=[r_kst])
                    else:
                        sc.op(V, lambda e, pm=pm, kst=kst, hh=hh: e.tensor_copy(out=kst[:, hh, :], in_=pm[:]), reads=[r_pm], writes=[r_kst])
                    cnt1["mi"] += 1
                    if nchunks == 16 and 3 <= g < 13:
                        if hh == 0:
                            rms_own(g - 3)
                        if hh == 2:
                            tr_own(g - 3)
                    if nchunks == 16 and 4 <= g < 8 and hh == 4:
                        wq = g - 4
                        sc.dma(G, "wocast%d" % wq, wobf[wq * 512:(wq + 1) * 512, :], w_out[wq * 512:(wq + 1) * 512, :], writes=[r_wobf[wq]])
                    if nchunks == 16 and 13 <= g < 15:
                        if hh == 0:
                            rms_mem(g - 13)
                        if hh == 2:
                            tr_mem(g - 13)
                    if hh % 2 == 1 and g + 1 < nchunks:
                        rms(g + 1, hh // 2)
                sc.dma(G, "kst0", kscr[:, :, g * 512:(g + 1) * 512].rearrange("h d t -> d h t"), kst[:],
                       reads=[r_kst], writes=[r_kchunk[g]])
                for i in range(4):
                    for cg in range(2):
                        mi = cnt1["mi"]
                        pm, r_pm = pmm[mi % 4], r_pmm[mi % 4]
                        for k in range(16):
                            sc.op(P, lambda e, k=k, i=i, cg=cg, pm=pm, hT=hT: e.matmul(out=pm[:], lhsT=hT[:, k, i * 128:(i + 1) * 128],
                                                                                       rhs=wv[:, k, cg * 512:(cg + 1) * 512],
                                                                                       start=(k == 0), stop=(k == 15)),
                                  reads=[r_wvg[cg], r_hT], writes=[r_pm], sig=(k == 15))
                        if mi % 2 == 0:
                            sc.op(A, lambda e, pm=pm, vst=vst, i=i, cg=cg: e.copy(out=vst[:, cg * 4:(cg + 1) * 4, i, 0:128],
                                                                                  in_=pm[:].rearrange("p (h d) -> p h d", d=128)),
                                  reads=[r_pm], writes=[r_vst])
                        else:
                            sc.op(V, lambda e, pm=pm, vst=vst, i=i, cg=cg: e.tensor_copy(out=vst[:, cg * 4:(cg + 1) * 4, i, 0:128],
                                                                                         in_=pm[:].rearrange("p (h d) -> p h d", d=128)),
                                  reads=[r_pm], writes=[r_vst])
                        cnt1["mi"] += 1
                    for k in range(16):
                        sc.op(P, lambda e, k=k, i=i, hT=hT: e.matmul(out=pz[:, i * 8:(i + 1) * 8], lhsT=hT[:, k, i * 128:(i + 1) * 128],
                                                                       rhs=wv[:, k, 1024:1032], start=(k == 0), stop=(k == 15)),
                              reads=[r_wvg[2], r_hT], writes=[r_pz], sig=(k == 15))
                    if g + 1 < nchunks:
                        tr(g + 1, i)
                sc.op(V, lambda e, g=g: e.tensor_tensor(out=zcol[:, g * 4:(g + 1) * 4, :],
                                                        in0=pz[:, 0:32].rearrange("p (i h) -> p i h", h=H),
                                                        in1=bf_rep[:, None, :].to_broadcast([128, 4, H]), op=ALU.add),
                      reads=[r_pz, r_bf], writes=[r_zcol])
                sc.dma(G, "vst%d" % (g % 2), vscr[:, :, g * 4:(g + 1) * 4, :].rearrange("h p i c -> p h (i c)"), vst[:].rearrange("p h i c -> p h (i c)"),
                       reads=[r_vst], writes=[r_vchunk[g]])
            if debug is not None and debug[0] == "p1":
                lg = (nchunks - 1) % 2
                sc.dma(G, "dbg", dbg[:, 0:8192], hTs[lg][0][:].rearrange("p k t -> p (k t)"), reads=[hTs[lg][1]])
                sc.dma(G, "dbg", dbg[:, 16384:16384 + 512], zcol[:].rearrange("p t h -> p (t h)"), reads=[r_zcol])

        def barrier():
            toks = sc.final_tokens()
            for e in sc.E.values():
                e.wait(toks)

        def evac_copy(i, dst, src, r_src, r_dst, scale=None):
            if i % 2 == 0:
                if scale is None:
                    sc.op(A, lambda e: e.copy(out=dst, in_=src), reads=[r_src], writes=[r_dst])
                else:
                    sc.op(A, lambda e: e.activation(out=dst, in_=src, func=AF.Copy, scale=scale), reads=[r_src], writes=[r_dst])
            else:
                if scale is None:
                    sc.op(V, lambda e: e.tensor_copy(out=dst, in_=src), reads=[r_src], writes=[r_dst])
                else:
                    sc.op(V, lambda e: e.tensor_scalar_mul(out=dst, in0=src, scalar1=scale), reads=[r_src], writes=[r_dst])

        barrier()
        if stage >= 3:
          with ExitStack() as L1:
            r_qscr = R(); r_gscr = R()
            ycT = sb(L1, [128, 4, 1024], BF16, "ycT"); r_ycT = R()
            ymT = sb(L1, [128, 4, 1024], BF16, "ymT"); r_ymT = R()
            biasall = sb(L1, [128, 288, H], F32, "biasall"); r_biasall = R()
            maskadd = sb(L1, [128, 8, 128], BF16, "maskadd"); r_mask = R()
            sc.dma(SY, "c1a", maskadd[:], maskadd_d, writes=[r_mask])
            with ExitStack() as mid:
                mqT = sb(mid, [128, 4, 1024], BF16, "mqT"); r_mqT = R()
                gmem = sb(mid, [128, NJ, 512], BF16, "gmem"); r_gmem = R()
                mkT = sb(mid, [128, 4, 256], BF16, "mkT"); r_mkT = R()
                mv = sb(mid, [128, 2, 4, 129], BF16, "mv"); r_mv = R()
                mid2 = ExitStack()
                uext = sb(mid2, [128, 4, NJ, 160], F32, "uext"); r_uext = R()
                gconv = sb(mid2, [128, 4, 1024], BF16, "gconv"); r_gconv = R()
                with ExitStack() as own:
                    hTo = sb(own, [128, 16, 1024], BF16, "hTo"); r_hToA = R(); r_hToB = R()
                    qT = sb(own, [128, H, 1024], BF16, "qT"); r_qT = R()
                    gfox = sb(own, [128, NJ, 1024], BF16, "gfox"); r_gfox = R()
                    hTh = sb(own, [128, 16, 256], BF16, "hTh"); r_hTh = R()
                    pT = [ps(own, [128, 8, 128], BF16, "pT") for _ in range(2)]; r_pT = [R(), R()]
                    pmm = [ps(own, [128, 512], F32, "pmm") for _ in range(4)]; r_pmm = [R() for _ in range(4)]
                    sc.dma(SY, "hToA", hTo[:, :, 0:512], hscr[:, :, 0:512], reads=r_hscr[0:4], writes=[r_hToA])
                    sc.dma(SY, "hToB", hTo[:, :, 512:1024], hscr[:, :, 512:1024], reads=r_hscr[4:8], writes=[r_hToB])
                    sc.dma(SY, "hTh", hTh[:], hscr[:, :, 1024:1280], reads=r_hscr[8:10], writes=[r_hTh])
                    with ExitStack() as ob_:
                        GW = 256
                        wbs = [(sb(ob_, [128, 16, GW], BF16, "wb"), R()) for _ in range(2)]
                        convw = sb(ob_, [128, 4, 31], F32, "convw"); r_convw = R()
                        convp = sb(ob_, [128, 3, 4], F32, "convp"); r_convp = R()
                        acc = sb(ob_, [128, 4, 1024], F32, "acc"); r_acc = [R() for _ in range(4)]
                        u2T = sb(ob_, [128, 4, 1024], BF16, "u2T"); r_u2T = R()
                        wpw = sb(ob_, [128, 4, 512], BF16, "wpw"); r_wpw = R()
                        sc.dma(SY, "c2a", convw[:], convw_d, writes=[r_convw])
                        sc.dma(SY, "c2b", convp[:], convp_d, writes=[r_convp])
                        accf = acc[:].rearrange("p c t -> p (c t)")
                        lcol = accf[:, 0:512]; r_lcol = R()
                        sA = accf[:, 512:1024].rearrange("p (t h) -> p t h", h=H); r_sA = R()
                        sB = accf[:, 1024:1536].rearrange("p (t h) -> p t h", h=H); r_sB = R()
                        tmpe = accf[:, 1536:2048].rearrange("p (h t) -> p h t", t=NT); r_tmpe = R()
                        negc = accf[:, 2048:2560].rearrange("p (t h) -> p t h", h=H); r_negc = R()
                        tsb = accf[:, 2560:3072].rearrange("p (t h) -> p t h", h=H); r_tsb = R()
                        sel = accf[:, 3072:3584].rearrange("p (j t) -> p j t", t=NT); r_sel = R()
                        eown = accf[:, 3584:3648].rearrange("p (j h) -> p j h", h=H); r_eown = R()
                        sc.dma(SY, "c1b", sel, sel_d, writes=[r_sel])
                        pW = pmm[2]; r_pW = r_pmm[2]
                        pTt = pmm[3]; r_pTt = r_pmm[3]
                        sc.op(A, lambda e: e.activation(out=lcol, in_=zcol[:].rearrange("p t h -> p (t h)"), func=AF.Exp, scale=-1.0),
                              reads=[r_zcol], writes=[r_lcol])
                        sc.op(A, lambda e: e.activation(out=lcol, in_=lcol, func=AF.Ln, bias=cst[:, 1:2], scale=1.0),
                              reads=[r_lcol, r_cst], writes=[r_lcol])
                        sc.op(P, lambda e: e.matmul(out=pW[:], lhsT=uinc[:], rhs=lcol, start=True, stop=True), reads=[r_uinc, r_lcol], writes=[r_pW])
                        sc.op(P, lambda e: e.matmul(out=pTt[:], lhsT=ones_f[:], rhs=lcol, start=True, stop=True), reads=[r_onesf, r_lcol], writes=[r_pTt])
                        sc.op(V, lambda e: e.tensor_copy(out=tsb.rearrange("p t h -> p (t h)"), in_=pTt[:]), reads=[r_pTt], writes=[r_tsb])
                        cur, r_cur = tsb, r_tsb
                        bufs = [(sA, r_sA), (sB, r_sB)]
                        for si, s_ in enumerate((1, 2, 4, 8, 16, 32)):
                            nxt, r_nxt = bufs[si % 2]
                            sc.op(V, lambda e, nxt=nxt, cur=cur, s_=s_: e.tensor_copy(out=nxt[:, 0:s_, :], in_=cur[:, 0:s_, :]), reads=[r_cur], writes=[r_nxt])
                            sc.op(V, lambda e, nxt=nxt, cur=cur, s_=s_: e.tensor_tensor(out=nxt[:, s_:NT, :], in0=cur[:, s_:NT, :], in1=cur[:, 0:NT - s_, :], op=ALU.add),
                                  reads=[r_cur], writes=[r_nxt])
                            cur, r_cur = nxt, r_nxt
                        einc, r_einc = cur, r_cur
                        sc.op(V, lambda e: e.tensor_tensor(out=negc.rearrange("p t h -> p (t h)"), in0=pW[:], in1=einc.rearrange("p t h -> p (t h)"), op=ALU.add),
                              reads=[r_pW, r_einc], writes=[r_negc])
                        sc.op(V, lambda e: e.tensor_tensor(out=negc, in0=negc, in1=tsb, op=ALU.subtract), reads=[r_negc, r_tsb], writes=[r_negc])
                        for j in range(NJ):
                            sc.op(V, lambda e, j=j: e.tensor_tensor(out=tmpe, in0=tsb.rearrange("p t h -> p h t"),
                                                                     in1=sel[:, j:j + 1, :].to_broadcast([128, H, NT]), op=ALU.mult),
                                  reads=[r_tsb, r_sel], writes=[r_tmpe])
                            sc.op(V, lambda e, j=j: e.reduce_sum(out=eown[:, j, :], in_=tmpe, axis=AX.X), reads=[r_tmpe], writes=[r_eown])
                        for j in range(NJ):
                            nk = 8 * j + 8
                            off = 4 * j * (j + 1)
                            sc.op(V, lambda e, j=j, nk=nk, off=off: e.tensor_tensor(out=biasall[:, off:off + nk, :], in0=negc[:, 0:nk, :],
                                                                                     in1=eown[:, j:j + 1, :].to_broadcast([128, nk, H]), op=ALU.subtract),
                                  reads=[r_negc, r_eown], writes=[r_biasall])
                        st = {"wi": 0, "mi": 0}

                        wb_rs = [[R(), R()], [R(), R()]]

                        def getw(c0, ncols):
                            si = st["wi"] % 2
                            wb = wbs[si][0]
                            for ct in range(ncols // 128):
                                sc.dma(G, "wb%d_%d" % (si, ct), wb[:, :, ct * 128:(ct + 1) * 128],
                                       w_in[:, c0 + ct * 128:c0 + (ct + 1) * 128].rearrange("(k p) c -> p k c", p=128),
                                       writes=[wb_rs[si][ct]])
                            st["wi"] += 1
                            return wb, wb_rs[si]

                        def fm_group(c0, evac, halo=False):
                            wb, r_wbs = getw(c0, GW)
                            for ct in range(GW // 128):
                                r_wb = r_wbs[ct]
                                chunks = [(hTo, r_hToA, 0, 512, 0), (hTo, r_hToB, 512, 512, 1)]
                                if halo:
                                    chunks.append((hTh, r_hTh, 0, 256, 2))
                                for (src, r_src, o, n, qc) in chunks:
                                    pm, r_pm = pmm[st["mi"] % 4], r_pmm[st["mi"] % 4]
                                    for k in range(16):
                                        sc.op(P, lambda e, k=k, ct=ct, pm=pm, src=src, o=o, n=n, wb=wb: e.matmul(
                                            out=pm[:, 0:n], lhsT=wb[:, k, ct * 128:(ct + 1) * 128], rhs=src[:, k, o:o + n],
                                            start=(k == 0), stop=(k == 15)), reads=[r_wb, r_src], writes=[r_pm], sig=(k == 15))
                                    evac(ct, qc, pm, r_pm, st["mi"])
                                    st["mi"] += 1

                        def tm_group(c0, ncols, evac):
                            wb, r_wbs = getw(c0, ncols)
                            for j in range(NJ):
                                pm, r_pm = pmm[st["mi"] % 4], r_pmm[st["mi"] % 4]
                                r_h = r_hToA if j < 4 else r_hToB
                                for k in range(16):
                                    sc.op(P, lambda e, k=k, j=j, pm=pm, wb=wb: e.matmul(
                                        out=pm[:, 0:ncols], lhsT=hTo[:, k, j * 128:(j + 1) * 128], rhs=wb[:, k, 0:ncols],
                                        start=(k == 0), stop=(k == 15)), reads=[r_wbs[0], r_wbs[1], r_h], writes=[r_pm], sig=(k == 15))
                                evac(j, pm, r_pm, st["mi"])
                                st["mi"] += 1

                        def udst(ct, qc):
                            if qc < 2:
                                return uext[:, ct, 4 * qc:4 * qc + 4, 32:160], (lambda pm: pm[:, 0:512].rearrange("p (j t) -> p j t", t=128))
                            return uext[:, ct, :, 0:32], (lambda pm: pm[:, 0:256].rearrange("p (j t) -> p j t", t=32))

                        for gi in range(2):
                            def ev_b(ct, qc, pm, r_pm, mi, gi=gi):
                                d, v = udst(gi * 2 + ct, qc)
                                sc.op(A, lambda e: e.activation(out=d, in_=v(pm), func=AF.Sigmoid), reads=[r_pm], writes=[r_uext])
                            fm_group(C_B + gi * GW, ev_b, halo=True)
                        for gi in range(2):
                            def ev_a(ct, qc, pm, r_pm, mi, gi=gi):
                                d, v = udst(gi * 2 + ct, qc)
                                sc.op(V, lambda e: e.tensor_tensor(out=d, in0=v(pm), in1=d, op=ALU.mult), reads=[r_pm, r_uext], writes=[r_uext])
                            fm_group(C_A + gi * GW, ev_a, halo=True)
                        uflat = uext[:].rearrange("p c j t -> p (c j t)")
                        for ct in range(4):
                            a_ct = acc[:, ct, :].rearrange("p (j t) -> p j t", t=128)
                            sc.op(V, lambda e, ct=ct, a_ct=a_ct: e.tensor_scalar(out=a_ct, in0=uext[:, ct, :, 2:130], scalar1=convw[:, ct, 0:1],
                                                                                 scalar2=convp[:, 0, ct:ct + 1], op0=ALU.mult, op1=ALU.add),
                                  reads=[r_uext, r_convw, r_convp, r_biasall], writes=[r_acc[ct]])
                            for k in range(1, 31):
                                sc.op(V, lambda e, ct=ct, k=k, a_ct=a_ct: e.scalar_tensor_tensor(out=a_ct, in0=uext[:, ct, :, 2 + k:130 + k],
                                                                                                  scalar=convw[:, ct, k:k + 1], in1=a_ct,
                                                                                                  op0=ALU.mult, op1=ALU.add),
                                      reads=[r_uext, r_convw, r_acc[ct]], writes=[r_acc[ct]])
                        for gi in range(2):
                            def ev_g(ct, qc, pm, r_pm, mi, gi=gi):
                                sc.op(A, lambda e: e.activation(out=gconv[:, gi * 2 + ct, qc * 512:(qc + 1) * 512], in_=pm[:], func=AF.Silu),
                                      reads=[r_pm], writes=[r_gconv])
                            fm_group(C_G + gi * GW, ev_g)
                        for gi in range(4):
                            def ev_q(ct, qc, pm, r_pm, mi, gi=gi):
                                evac_copy(0, qT[:, gi * 2 + ct, qc * 512:(qc + 1) * 512], pm[:], r_pm, r_qT, scale=SCALE)
                            fm_group(C_Q + gi * GW, ev_q)
                        for gi in range(2):
                            def ev_mq(ct, qc, pm, r_pm, mi, gi=gi):
                                evac_copy(0, mqT[:, gi * 2 + ct, qc * 512:(qc + 1) * 512], pm[:], r_pm, r_mqT, scale=SCALE)
                            fm_group(C_MQ + gi * GW, ev_mq)
                        for gi in range(4):
                            def ev_fg(j, pm, r_pm, mi, gi=gi):
                                sc.op(A, lambda e: e.activation(out=gfox[:, j, gi * GW:(gi + 1) * GW], in_=pm[:, 0:GW], func=AF.Silu),
                                      reads=[r_pm], writes=[r_gfox])
                            tm_group(C_FG + gi * GW, GW, ev_fg)
                        for ct in range(4):
                            sc.op(A, lambda e, ct=ct: e.activation(out=uflat[:, ct * 1024:(ct + 1) * 1024], in_=acc[:, ct, :], func=AF.Square),
                                  reads=[r_acc[ct]], writes=[r_uext])
                        for gi in range(2):
                            def ev_mg(j, pm, r_pm, mi, gi=gi):
                                sc.op(A, lambda e: e.activation(out=gmem[:, j, gi * GW:(gi + 1) * GW], in_=pm[:, 0:GW], func=AF.Silu),
                                      reads=[r_pm], writes=[r_gmem])
                            tm_group(C_MG + gi * GW, GW, ev_mg)

                        sc.dma(SY, "qspill", qscr.rearrange("h d t -> d h t"), qT[:], reads=[r_qT], writes=[r_qscr])
                        sc.dma(SY, "gspill", gscr.rearrange("h p j d -> p j h d"), gfox[:].rearrange("p j (h d) -> p j h d", d=128),
                               reads=[r_gfox], writes=[r_gscr])
                        wmkv, r_wmkv = hTo, R()
                        hmT, r_hmT = wbs[0][0], R()
                        r_wmv = R()
                        hto_readers = [t for r_ in (r_hToA, r_hToB) for t in ([r_.w] + list(r_.rs.values()))]
                        sc.dma(G, "wmk", wmkv[:, :, 0:512], w_mkv[:, 0:512].rearrange("(k p) c -> p k c", p=128), writes=[r_wmkv, r_hToA, r_hToB])
                        sc.dma(G, "wmv", wmkv[:, :, 512:1024], w_mkv[:, 512:1024].rearrange("(k p) c -> p k c", p=128), writes=[r_wmv], after=hto_readers)
                        sc.dma(SY, "hmT", hmT[:], hmscr, reads=r_hmscr, writes=[r_hmT, wb_rs[0][0], wb_rs[0][1]])
                        sc.op(G, lambda e: e.memset(mv[:], 1.0), writes=[r_mv])
                        wload(wpw, r_wpw, "wpw", 0, 512, src=w_pw, rows=512)
                        mr = hTh[:].rearrange("p k t -> p (k t)").bitcast(F32)
                        mean = mr[:, 0:1024]
                        rstd = mr[:, 1024:2048]
                        r_mean = r_hTh
                        r_rstd = r_hTh
                        for qc in range(2):
                            p1_, rp1 = pmm[qc], r_pmm[qc]
                            p2_, rp2 = pmm[2 + qc], r_pmm[2 + qc]
                            for ct in range(4):
                                sc.op(P, lambda e, ct=ct, qc=qc, p1_=p1_: e.matmul(out=p1_[:], lhsT=ones_f[:], rhs=acc[:, ct, qc * 512:(qc + 1) * 512],
                                                                                   start=(ct == 0), stop=(ct == 3)), reads=[r_onesf, r_acc[ct]], writes=[rp1], sig=(ct == 3))
                            for ct in range(4):
                                sc.op(P, lambda e, ct=ct, qc=qc, p2_=p2_: e.matmul(out=p2_[:], lhsT=ones_f[:], rhs=uflat[:, ct * 1024 + qc * 512:ct * 1024 + (qc + 1) * 512],
                                                                                   start=(ct == 0), stop=(ct == 3)), reads=[r_onesf, r_uext], writes=[rp2], sig=(ct == 3))
                            sl = slice(qc * 512, (qc + 1) * 512)
                            sc.op(V, lambda e, p1_=p1_, sl=sl: e.tensor_scalar_mul(out=mean[:, sl], in0=p1_[:], scalar1=1.0 / 512), reads=[rp1], writes=[r_mean])
                            sc.op(V, lambda e, sl=sl: e.tensor_tensor(out=rstd[:, sl], in0=mean[:, sl], in1=mean[:, sl], op=ALU.mult), reads=[r_mean], writes=[r_rstd])
                            sc.op(V, lambda e, p2_=p2_, sl=sl: e.scalar_tensor_tensor(out=rstd[:, sl], in0=p2_[:], scalar=1.0 / 512, in1=rstd[:, sl],
                                                                                      op0=ALU.mult, op1=ALU.subtract), reads=[rp2, r_rstd], writes=[r_rstd])
                            sc.op(A, lambda e, sl=sl: e.activation(out=rstd[:, sl], in_=rstd[:, sl], func=AF.Sqrt, bias=cst[:, 0:1], scale=1.0),
                                  reads=[r_rstd, r_cst], writes=[r_rstd])
                            sc.op(V, lambda e, sl=sl: e.reciprocal(out=rstd[:, sl], in_=rstd[:, sl]), reads=[r_rstd], writes=[r_rstd])
                        for ct in range(4):
                            sc.op(V, lambda e, ct=ct: e.tensor_tensor(out=acc[:, ct, :], in0=acc[:, ct, :], in1=mean, op=ALU.subtract),
                                  reads=[r_acc[ct], r_mean], writes=[r_acc[ct]])
                            sc.op(V, lambda e, ct=ct: e.tensor_tensor(out=acc[:, ct, :], in0=acc[:, ct, :], in1=rstd, op=ALU.mult),
                                  reads=[r_acc[ct], r_rstd], writes=[r_acc[ct]])
                            sc.op(A, lambda e, ct=ct: e.activation(out=u2T[:, ct, :], in_=acc[:, ct, :], func=AF.Silu, bias=convp[:, 2, ct:ct + 1],
                                                                  scale=convp[:, 1, ct:ct + 1]), reads=[r_acc[ct], r_convp], writes=[r_u2T])
                        mi2 = 0
                        for co in range(4):
                            for qc in range(2):
                                pm, r_pm = pmm[mi2 % 4], r_pmm[mi2 % 4]
                                for ci in range(4):
                                    sc.op(P, lambda e, ci=ci, co=co, qc=qc, pm=pm: e.matmul(out=pm[:], lhsT=wpw[:, ci, co * 128:(co + 1) * 128],
                                                                                            rhs=u2T[:, ci, qc * 512:(qc + 1) * 512], start=(ci == 0), stop=(ci == 3)),
                                          reads=[r_wpw, r_u2T], writes=[r_pm], sig=(ci == 3))
                                sc.op(V, lambda e, co=co, qc=qc, pm=pm: e.tensor_tensor(out=ycT[:, co, qc * 512:(qc + 1) * 512], in0=pm[:],
                                                                                        in1=gconv[:, co, qc * 512:(qc + 1) * 512], op=ALU.mult),
                                      reads=[r_pm, r_gconv], writes=[r_ycT])
                                mi2 += 1
                        for hh in range(4):
                            pm, r_pm = pmm[hh % 4], r_pmm[hh % 4]
                            for k in range(16):
                                sc.op(P, lambda e, k=k, hh=hh, pm=pm: e.matmul(out=pm[:, 0:256], lhsT=wmkv[:, k, hh * 128:(hh + 1) * 128], rhs=hmT[:, k, :],
                                                                               start=(k == 0), stop=(k == 15)), reads=[r_wmkv, r_hmT], writes=[r_pm], sig=(k == 15))
                            evac_copy(hh, mkT[:, hh, :], pm[:, 0:256], r_pm, r_mkT)
                        for mt in range(2):
                            pm, r_pm = pmm[mt % 4], r_pmm[mt % 4]
                            for k in range(16):
                                sc.op(P, lambda e, k=k, mt=mt, pm=pm: e.matmul(out=pm[:], lhsT=hmT[:, k, mt * 128:(mt + 1) * 128], rhs=wmkv[:, k, 512:1024],
                                                                               start=(k == 0), stop=(k == 15)), reads=[r_wmv, r_hmT], writes=[r_pm], sig=(k == 15))
                            evac_copy(mt, mv[:, mt, :, 0:128], pm[:].rearrange("p (h d) -> p h d", d=128), r_pm, r_mv)
                        dead_acc = [t for ct in range(4) for t in ([r_acc[ct].w] + list(r_acc[ct].rs.values()))]
                        dead_u2T = [r_u2T.w] + list(r_u2T.rs.values())
                        ymem = accf[:, 0:2048].bitcast(BF16).rearrange("p (j c) -> p j c", c=512); r_ymem = R()
                        u2f = u2T[:].rearrange("p c t -> p (c t)")
                        pTm = [u2f[:, i * 512:(i + 1) * 512] for i in range(4)]; r_pTm = [R() for _ in range(4)]
                        rec = sb(ob_, [128, 4], F32, "recm"); r_rec = R()
                        units = [(hh, qc) for hh in range(4) for qc in range(2)]

                        def mem_S(u):
                            hh, qc = units[u]
                            for mt in range(2):
                                ps_, rps = pmm[mt], r_pmm[mt]
                                tm, rtm = pTm[(u % 2) * 2 + mt], r_pTm[(u % 2) * 2 + mt]
                                sc.op(P, lambda e, ps_=ps_, mt=mt: e.matmul(out=ps_[:], lhsT=mkT[:, hh, mt * 128:(mt + 1) * 128],
                                                                            rhs=mqT[:, hh, qc * 512:(qc + 1) * 512], start=True, stop=True),
                                      reads=[r_mkT, r_mqT], writes=[rps])
                                sc.op(A, lambda e, ps_=ps_, tm=tm: e.activation(out=tm, in_=ps_[:], func=AF.Exp), reads=[rps], writes=[rtm],
                                      after=(dead_u2T if u < 2 else ()))

                        def mem_PV(u, ai0):
                            hh, qc = units[u]
                            for jq in range(4):
                                j = qc * 4 + jq
                                ai = ai0 + jq
                                pa, r_pa = pmm[2 + ai % 2], r_pmm[2 + ai % 2]
                                for mt in range(2):
                                    tm, rtm = pTm[(u % 2) * 2 + mt], r_pTm[(u % 2) * 2 + mt]
                                    sc.op(P, lambda e, mt=mt, tm=tm, pa=pa, jq=jq: e.matmul(out=pa[:, 0:129], lhsT=tm[:, jq * 128:(jq + 1) * 128],
                                                                                     rhs=mv[:, mt, hh, :], start=(mt == 0), stop=(mt == 1)),
                                          reads=[rtm, r_mv], writes=[r_pa], sig=(mt == 1))
                                rc = rec[:, (ai % 4):(ai % 4) + 1]
                                sc.op(V, lambda e, pa=pa, rc=rc: e.reciprocal(out=rc, in_=pa[:, 128:129]), reads=[r_pa], writes=[r_rec])
                                sc.op(V, lambda e, pa=pa, rc=rc, j=j: e.scalar_tensor_tensor(out=ymem[:, j, hh * 128:(hh + 1) * 128], in0=pa[:, 0:128],
                                                                                             scalar=rc, in1=gmem[:, j, hh * 128:(hh + 1) * 128],
                                                                                             op0=ALU.mult, op1=ALU.mult),
                                      reads=[r_pa, r_rec, r_gmem], writes=[r_ymem], after=(dead_acc if (u == 0 and jq == 0) else ()))

                        mem_S(0)
                        for u in range(len(units)):
                            if u + 1 < len(units):
                                mem_S(u + 1)
                            mem_PV(u, u * 4)
                        for j in range(NJ):
                            pt, r_pt = pT[j % 2], r_pT[j % 2]
                            for hh in range(4):
                                sc.op(P, lambda e, j=j, hh=hh, pt=pt: e.transpose(out=pt[:, hh, :], in_=ymem[:, j, hh * 128:(hh + 1) * 128], identity=ident[:]),
                                      reads=[r_ymem, r_ident], writes=[r_pt], sig=(hh == 3))
                            evac_copy(j, ymT[:, :, j * 128:(j + 1) * 128], pt[:, 0:4, :], r_pt, r_ymT)
                        barrier()
                mid2.close()
            with ExitStack() as atO:
                yfT = sb(atO, [128, H, 1024], BF16, "yfT"); r_yfT = R()
                woA = sb(atO, [128, 8, D], BF16, "woA"); r_woA = R()
                woB = sb(atO, [128, 8, D], BF16, "woB"); r_woB = R()
                with ExitStack() as at:
                    KT = [sb(at, [128, S], BF16, "KT") for _ in range(2)]; r_KT = [[R() for _ in range(4)] for _ in range(2)]
                    Vh = [sb(at, [128, NT, 129], BF16, "Vh") for _ in range(2)]; r_Vh = [[R(), R()] for _ in range(2)]
                    Sb = [sb(at, [128, 4, 128], F32, "Sb") for _ in range(3)]; r_Sb = [R() for _ in range(3)]
                    pTb = [sb(at, [128, 4, 128], BF16, "pTb") for _ in range(4)]; r_pTb = [R() for _ in range(4)]
                    yfox = [sb(at, [128, 128], BF16, "yfox") for _ in range(2)]; r_yfox = [R(), R()]
                    rec = sb(at, [128, 4], F32, "rec"); r_rec = R()
                    ytmp = [sb(at, [128, 128], F32, "ytmp") for _ in range(2)]; r_ytmp = [R(), R()]
                    qh = [sb(at, [128, 1024], BF16, "qh") for _ in range(2)]; r_qh = [R(), R()]
                    mbias = [sb(at, [128, 4, 128], F32, "mbias") for _ in range(2)]; r_mbias = [R() for _ in range(2)]
                    gh = [sb(at, [128, NJ, 128], BF16, "gh") for _ in range(2)]; r_gh = [R(), R()]
                    pS = [ps(at, [128, 4, 128], F32, "pS") for _ in range(5)]; r_pS = [R() for _ in range(5)]
                    pacc = [ps(at, [128, 512], F32, "pacc") for _ in range(2)]; r_pacc = [R(), R()]
                    pTy = ps(at, [128, 8, 128], BF16, "pTy"); r_pTy = R()
                    nheads = H if stage >= 5 else 1

                    def load_head(hh):
                        b = hh % 2
                        def ld_k(a):
                            sc.dma(SY, "KT%dq%d" % (b, a), KT[b][:, a * 2048:(a + 1) * 2048], kscr[hh, :, a * 2048:(a + 1) * 2048],
                                   reads=r_kchunk[a * 4:(a + 1) * 4], writes=[r_KT[b][a]])

                        def ld_v(a):
                            sc.dma(SY, "Vh%dh%d" % (b, a), Vh[b][:, a * 32:(a + 1) * 32, :], vscr[hh, :, a * 32:(a + 1) * 32, :],
                                   reads=r_vchunk[a * 8:(a + 1) * 8], writes=[r_Vh[b][a]])

                        sc.dma(SY, "qh%d" % b, qh[b][:], qscr[hh], reads=[r_qscr], writes=[r_qh[b]])
                        ld_k(0)
                        ld_v(0)
                        sc.dma(SY, "gh%d" % b, gh[b][:], gscr[hh], reads=[r_gscr], writes=[r_gh[b]])
                        ld_k(1)
                        ld_k(2)
                        ld_k(3)
                        ld_v(1)

                    groups = []
                    for hh in range(nheads):
                        for j in range(NJ):
                            nk = 8 * j + 8
                            for gk in range(nk // 4):
                                groups.append((hh, j, gk, nk))

                    def emit_S(gi):
                        hh, j, gk, nk = groups[gi]
                        b = hh % 2
                        s4 = gi % 5
                        for i in range(4):
                            kt = gk * 4 + i
                            sc.op(P, lambda e, i=i, kt=kt, hh=hh, j=j, b=b, s4=s4: e.matmul(out=pS[s4][:, i, :], lhsT=KT[b][:, kt * 128:(kt + 1) * 128],
                                                                                             rhs=qh[b][:, j * 128:(j + 1) * 128], start=True, stop=True),
                                  reads=[r_KT[b][kt // 16], r_qh[b]], writes=[r_pS[s4]], sig=(i == 3))

                    def emit_soft(gi):
                        hh, j, gk, nk = groups[gi]
                        s4 = gi % 5
                        p4 = gi % 4
                        s3 = gi % 3
                        off = 4 * j * (j + 1)
                        diag = gk >= 2 * j
                        kt0 = gk * 4
                        if (not diag) and gk % 7 == 99:
                            for i in range(4):
                                sc.op(A, lambda e, i=i: e.activation(out=pTb[p4][:, i, :], in_=pS[s4][:, i, :], func=AF.Exp,
                                                                     bias=biasall[:, off + kt0 + i, hh:hh + 1], scale=1.0),
                                      reads=[r_pS[s4], r_biasall], writes=[r_pTb[p4]])
                            return
                        if diag and gi in dslot:
                            mb_, r_mb = mbias[dslot[gi] % 2], r_mbias[dslot[gi] % 2]
                            sc.op(V, lambda e: e.tensor_tensor(out=Sb[s3][:], in0=pS[s4][:], in1=mb_[:], op=ALU.add),
                                  reads=[r_pS[s4], r_mb], writes=[r_Sb[s3]])
                        elif diag:
                            m0 = kt0 - 8 * j
                            sc.op(V, lambda e: e.tensor_tensor(
                                out=Sb[s3][:], in0=pS[s4][:], in1=biasall[:, off + kt0:off + kt0 + 4, hh:hh + 1].to_broadcast([128, 4, 128]), op=ALU.add),
                                reads=[r_pS[s4], r_biasall], writes=[r_Sb[s3]])
                            sc.op(V, lambda e: e.tensor_tensor(out=Sb[s3][:], in0=Sb[s3][:], in1=maskadd[:, m0:m0 + 4, :], op=ALU.add),
                                  reads=[r_Sb[s3], r_mask], writes=[r_Sb[s3]])
                        else:
                            sc.op(V, lambda e: e.tensor_tensor(
                                out=Sb[s3][:], in0=pS[s4][:], in1=biasall[:, off + kt0:off + kt0 + 4, hh:hh + 1].to_broadcast([128, 4, 128]), op=ALU.add),
                                reads=[r_pS[s4], r_biasall], writes=[r_Sb[s3]])
                        sc.op(A, lambda e: e.activation(out=pTb[p4][:], in_=Sb[s3][:], func=AF.Exp), reads=[r_Sb[s3]], writes=[r_pTb[p4]])

                    dslot = {}
                    dlist = []
                    for gi_, (hh_, j_, gk_, nk_) in enumerate(groups):
                        if gk_ >= 2 * j_:
                            dslot[gi_] = len(dlist)
                            dlist.append(gi_)

                    def emit_mb(di_):
                        gi_ = dlist[di_]
                        hh, j, gk, nk = groups[gi_]
                        off = 4 * j * (j + 1)
                        kt0 = gk * 4
                        m0 = kt0 - 8 * j
                        mb_, r_mb = mbias[di_ % 2], r_mbias[di_ % 2]
                        sc.op(G, lambda e: e.tensor_tensor(out=mb_[:], in0=maskadd[:, m0:m0 + 4, :],
                                                           in1=biasall[:, off + kt0:off + kt0 + 4, hh:hh + 1].to_broadcast([128, 4, 128]), op=ALU.add),
                              reads=[r_mask, r_biasall], writes=[r_mb])

                    def emit_PV(gi):
                        hh, j, gk, nk = groups[gi]
                        b = hh % 2
                        p4 = gi % 4
                        pa, r_pa = pacc[(hh * NJ + j) % 2], r_pacc[(hh * NJ + j) % 2]
                        for i in range(4):
                            kt = gk * 4 + i
                            sc.op(P, lambda e, i=i, kt=kt: e.matmul(out=pa[:, 0:129], lhsT=pTb[p4][:, i, :], rhs=Vh[b][:, kt, :],
                                                                   start=(kt == 0), stop=(kt == nk - 1)),
                                  reads=[r_pTb[p4], r_Vh[b][kt // 32]], writes=[r_pa], sig=(i == 3))

                    epi = {"ci": 0}

                    def emit_epi(gi):
                        hh, j, gk, nk = groups[gi]
                        ci = epi["ci"]
                        epi["ci"] += 1
                        pa, r_pa = pacc[(hh * NJ + j) % 2], r_pacc[(hh * NJ + j) % 2]
                        rc = rec[:, (ci % 4):(ci % 4) + 1]
                        yf, r_yf = yfox[ci % 2], r_yfox[ci % 2]
                        sc.op(V, lambda e: e.reciprocal(out=rc, in_=pa[:, 128:129]), reads=[r_pa], writes=[r_rec])
                        yt, r_yt = ytmp[ci % 2], r_ytmp[ci % 2]
                        sc.op(A, lambda e: e.activation(out=yt[:], in_=pa[:, 0:128], func=AF.Copy, scale=rc), reads=[r_pa, r_rec], writes=[r_yt])
                        sc.op(G, lambda e: e.tensor_tensor(out=yf[:], in0=yt[:], in1=gh[hh % 2][:, j, :], op=ALU.mult),
                              reads=[r_yt, r_gh[hh % 2]], writes=[r_yf])
                        sc.op(P, lambda e: e.transpose(out=pTy[:, ci % 8, :], in_=yf[:], identity=ident[:]), reads=[r_yf, r_ident], writes=[r_pTy])
                        sc.op(A, lambda e: e.copy(out=yfT[:, hh, j * 128:(j + 1) * 128], in_=pTy[:, ci % 8, :]), reads=[r_pTy], writes=[r_yfT])

                    def is_last(gi):
                        hh, j, gk, nk = groups[gi]
                        return gk == nk // 4 - 1

                    load_head(0)
                    emit_S(0)
                    emit_S(1)
                    emit_S(2)
                    ng = len(groups)
                    emit_mb(0)
                    for gi in range(ng + 3):
                        if gi in dslot and dslot[gi] + 1 < len(dlist):
                            emit_mb(dslot[gi] + 1)
                        if gi < ng:
                            if gi + 3 < ng:
                                emit_S(gi + 3)
                            emit_soft(gi)
                        if 0 <= gi - 1 < ng:
                            emit_PV(gi - 1)
                        if gi < ng:
                            hh, j, gk, nk = groups[gi]
                            if j == 1 and gk == 1 and hh + 1 < nheads:
                                load_head(hh + 1)
                                if hh == min(3, nheads - 2):
                                    for wq in range(4):
                                        wdst, rw, key = (woA, r_woA, "woA") if wq < 2 else (woB, r_woB, "woB")
                                        a4 = (wq % 2) * 4
                                        sc.dma(SY, key, wdst[:, a4:a4 + 4, :], wobf[wq * 512:(wq + 1) * 512, :].rearrange("(k p) c -> p k c", p=128),
                                               reads=[r_wobf[wq]], writes=[rw])
                        if 0 <= gi - 3 < ng and is_last(gi - 3):
                            emit_epi(gi - 3)
                    barrier()
                with ExitStack() as op_:
                    fg_rep = sb(op_, [128, D], F32, "fgrep"); r_fg = R()
                    xos = [(sb(op_, [128, D], F32, "xo"), R(), sb(op_, [128, 4], F32, "sso"), R()) for _ in range(2)]
                    obuf = sb(op_, [128, D], F32, "obuf"); r_obuf = R()
                    pmo = [ps(op_, [128, 512], F32, "pmo") for _ in range(4)]; r_pmo = [R() for _ in range(4)]
                    sc.dma(SY, "c4", fg_rep[:], fg_rep_d, writes=[r_fg])

                    def ysrc(mt, j):
                        if mt < 4:
                            return ycT[:, mt, j * 128:(j + 1) * 128], r_ycT
                        if mt < 12:
                            return yfT[:, mt - 4, j * 128:(j + 1) * 128], r_yfT
                        return ymT[:, mt - 12, j * 128:(j + 1) * 128], r_ymT
                    mi = 0
                    for j in range(NJ):
                        xo, r_xo, sso, r_sso = xos[j % 2]
                        sc.dma(SY, "xo%d" % (j % 2), xo[:], x_own[j * 128:(j + 1) * 128, :], writes=[r_xo])
                        for ncg in range(4):
                            pm, r_pm = pmo[mi % 4], r_pmo[mi % 4]
                            for mt in range(16):
                                ya, r_ya = ysrc(mt, j)
                                wsrc, r_ws = (woA, r_woA) if mt < 8 else (woB, r_woB)
                                sc.op(P, lambda e, mt=mt, ncg=ncg, pm=pm, ya=ya, wsrc=wsrc: e.matmul(out=pm[:], lhsT=ya, rhs=wsrc[:, mt % 8, ncg * 512:(ncg + 1) * 512],
                                                                                                     start=(mt == 0), stop=(mt == 15)),
                                      reads=[r_ya, r_ws], writes=[r_pm], sig=(mt == 15))
                            sc.op(V, lambda e, ncg=ncg, pm=pm, xo=xo: e.tensor_tensor(out=xo[:, ncg * 512:(ncg + 1) * 512], in0=pm[:],
                                                                                     in1=xo[:, ncg * 512:(ncg + 1) * 512], op=ALU.add),
                                  reads=[r_pm, r_xo], writes=[r_xo])
                            mi += 1
                        sc.op(A, lambda e, xo=xo, sso=sso: e.activation(out=obuf[:], in_=xo[:], func=AF.Square, accum_out=sso[:, 0:1]),
                              reads=[r_xo], writes=[r_obuf, r_sso])
                        sc.op(A, lambda e, sso=sso: e.activation(out=sso[:, 1:2], in_=sso[:, 0:1], func=AF.Sqrt, bias=cst[:, 0:1], scale=1.0 / D),
                              reads=[r_sso, r_cst], writes=[r_sso])
                        sc.op(V, lambda e, sso=sso: e.reciprocal(out=sso[:, 2:3], in_=sso[:, 1:2]), reads=[r_sso], writes=[r_sso])
                        sc.op(V, lambda e, xo=xo, sso=sso: e.scalar_tensor_tensor(out=obuf[:], in0=xo[:], scalar=sso[:, 2:3], in1=fg_rep[:],
                                                                                  op0=ALU.mult, op1=ALU.mult),
                              reads=[r_xo, r_sso, r_fg], writes=[r_obuf])
                        sc.dma(SY, "obuf", out[j * 128:(j + 1) * 128, :], obuf[:], reads=[r_obuf])
                    barrier()

        toks = sc.final_tokens()
        sc.E[SY].wait(toks)

        with nc.Block() as block:
            @block.sync
            def _(eng):
                sc.replay(SY, eng)

            @block.scalar
            def _(eng):
                sc.replay(A, eng)

            @block.vector
            def _(eng):
                sc.replay(V, eng)

            @block.gpsimd
            def _(eng):
                sc.replay(G, eng)

            @block.tensor
            def _(eng):
                sc.replay(P, eng)
    return nc


def prep_inputs(inputs, c):
    f32 = np.float32
    x = np.asarray(inputs["x"], f32)[0]
    tiles = [c + 8 * j for j in range(NJ)]
    x_own = np.concatenate([x[t * 128:(t + 1) * 128] for t in tiles], axis=0)
    x_halo = np.zeros((256, D), f32)
    for j, t in enumerate(tiles):
        if t > 0:
            x_halo[j * 32:(j + 1) * 32] = x[t * 128 - 32:t * 128]
    rep = lambda v: np.ascontiguousarray(np.broadcast_to(np.asarray(v, f32).reshape(1, -1), (128, np.asarray(v).size)))
    conv_w = np.asarray(inputs["conv_w"], f32)[0]
    conv_w_t = np.ascontiguousarray(conv_w.T.reshape(4, 128, 31).transpose(1, 0, 2))
    par = np.stack([np.asarray(inputs["conv_b"], f32)[0], np.asarray(inputs["conv_ln_g"], f32)[0],
                    np.asarray(inputs["conv_ln_b"], f32)[0]])
    conv_par = np.ascontiguousarray(par.reshape(3, 4, 128).transpose(2, 0, 1))
    k = np.arange(128)
    uinc = (k[:, None] <= k[None, :]).astype(f32)
    m = np.arange(8)
    val = 128 * (m[None, :, None] - c) + k[:, None, None] - k[None, None, :]
    maskadd = np.where(val <= 0, 0.0, NEG).astype(f32)
    tt = np.arange(NT)
    sel = (tt[None, :] < (c + 8 * np.arange(NJ))[:, None]).astype(f32)
    sel = np.ascontiguousarray(np.broadcast_to(sel[None], (128, NJ, NT)))
    return {
        "x_all": x, "x_own": x_own, "x_halo": x_halo, "mem": np.asarray(inputs["mem"], f32)[0],
        "norm_g_rep": rep(inputs["norm_g"]), "mem_norm_g_rep": rep(inputs["mem_norm_g"]),
        "final_g_rep": rep(inputs["final_g"]), "w_in": np.asarray(inputs["w_in"], f32)[0],
        "b_f_rep": rep(inputs["b_f"]), "conv_w_t": conv_w_t, "conv_par": conv_par,
        "w_conv_pw": np.asarray(inputs["w_conv_pw"], f32)[0], "w_mem_kv": np.asarray(inputs["w_mem_kv"], f32)[0],
        "w_out": np.asarray(inputs["w_out"], f32)[0],
        "ident": np.eye(128, dtype=f32).astype(ml_dtypes.bfloat16), "uinc": uinc, "maskadd": maskadd.astype(ml_dtypes.bfloat16), "sel": sel,
    }


def kernel(**inputs):
    nc = build_nc()
    in_maps = [prep_inputs(inputs, c) for c in range(NCORES)]
    res = run_bass_kernel_spmd(nc, in_maps, core_ids=list(range(NCORES)))
    outp = np.zeros((1, S, D), np.float32)
    for c in range(NCORES):
        o = np.asarray(res.results[c]["out"], np.float32)
        for j in range(NJ):
            t = c + 8 * j
            outp[0, t * 128:(t + 1) * 128] = o[j * 128:(j + 1) * 128]
    return outp
```

```python
from contextlib import ExitStack

import numpy as np
import ml_dtypes
import concourse.bass as bass
import concourse.mybir as mybir
from concourse.bass_utils import run_bass_kernel_spmd

F32 = mybir.dt.float32
BF16 = mybir.dt.bfloat16
AF = mybir.ActivationFunctionType
ALU = mybir.AluOpType
AX = mybir.AxisListType

NCORES = 8
S = 8192
D = 2048
NT = S // 128
NJ = 8
H = 8
DIN = 6664
EPS = 1e-6
SCALE = 128.0 ** -0.5
C_A, C_B, C_G = 0, 512, 1024
C_Q, C_K, C_V, C_F, C_FG, C_MQ, C_MG = 1536, 2560, 3584, 4608, 4616, 5640, 6152
NEG = -1.0e30
STAGE = 99
ACT_ONLY_OF_9 = 5
DEBUG = None


class R:
    __slots__ = ("w", "rs")

    def __init__(self):
        self.w = None
        self.rs = {}


class EngState:
    def __init__(self, sched, name):
        self.sched = sched
        self.name = name
        self.prog = []
        self.sem = sched.new_sem(name + "_s0")
        self.nsem = 1
        self.cnt = 0
        self.seen = {}

    def wait(self, toks):
        best = {}
        for t in toks:
            if t is None:
                continue
            sem, val = t
            if sem is self.sem and val > self.cnt:
                continue
            k = id(sem)
            if self.seen.get(k, 0) >= val:
                continue
            if k not in best or best[k][1] < val:
                best[k] = (sem, val)
        for k, (sem, val) in best.items():
            self.seen[k] = val
            self.prog.append(("wait", sem, val))

    def emit(self, fn, sig, inc=1, sem=None):
        if sem is None:
            if sig:
                if self.cnt >= 16000:
                    self.sem = self.sched.new_sem("%s_s%d" % (self.name, self.nsem))
                    self.nsem += 1
                    self.cnt = 0
                self.cnt += inc
                self.prog.append(("op", fn, self.sem, inc))
                return (self.sem, self.cnt)
            self.prog.append(("op", fn, None, 0))
            return (self.sem, self.cnt + 1)
        self.prog.append(("op", fn, sem, inc))
        return None


class Sched:
    def __init__(self, nc, es):
        self.nc = nc
        self.es = es
        self.nsems = 0
        self.E = {n: EngState(self, n) for n in ("sync", "scalar", "vector", "gpsimd", "tensor")}
        self.dsem = {}

    def new_sem(self, name):
        self.nsems += 1
        return self.es.enter_context(self.nc.semaphore(name))

    def op(self, ename, fn, reads=(), writes=(), sig=True, after=()):
        e = self.E[ename]
        deps = list(after)
        for r in reads:
            deps.append(r.w)
        for w in writes:
            deps.append(w.w)
            deps.extend(w.rs.values())
        e.wait(deps)
        tok = e.emit(fn, sig)
        self._upd(tok, reads, writes)
        return tok

    def _upd(self, tok, reads, writes):
        k = id(tok[0])
        for r in reads:
            if k not in r.rs or r.rs[k][1] < tok[1]:
                r.rs[k] = tok
        for w in writes:
            w.w = tok
            w.rs = {}

    def dma(self, qname, key, out, in_, reads=(), writes=(), after=()):
        e = self.E[qname]
        deps = list(after)
        for r in reads:
            deps.append(r.w)
        for w in writes:
            deps.append(w.w)
            deps.extend(w.rs.values())
        e.wait(deps)
        if key not in self.dsem:
            self.dsem[key] = [self.new_sem("d_" + key), 0]
        ds = self.dsem[key]
        ds[1] += 16
        e.emit(lambda eng: eng.dma_start(out=out, in_=in_), True, 16, sem=ds[0])
        tok = (ds[0], ds[1])
        self._upd(tok, reads, writes)
        return tok

    def replay(self, ename, eng):
        for it in self.E[ename].prog:
            if it[0] == "wait":
                eng.wait_ge(it[1], it[2])
            else:
                inst = it[1](eng)
                if it[2] is not None:
                    inst.then_inc(it[2], it[3])

    def final_tokens(self):
        toks = []
        for e in self.E.values():
            if e.cnt > 0:
                toks.append((e.sem, e.cnt))
        for ds in self.dsem.values():
            toks.append((ds[0], ds[1]))
        return toks


def build_nc(stage=STAGE, debug=DEBUG):
    nc = bass.Bass("TRN2", target_bir_lowering=False)

    def din(name, shape, dt=F32):
        return nc.dram_tensor(name, list(shape), dt, kind="ExternalInput").ap()

    x_all = din("x_all", [S, D])
    x_own = din("x_own", [NJ * 128, D])
    x_halo = din("x_halo", [256, D])
    mem = din("mem", [256, D])
    g_rep_d = din("norm_g_rep", [128, D])
    mg_rep_d = din("mem_norm_g_rep", [128, D])
    fg_rep_d = din("final_g_rep", [128, D])
    w_in = din("w_in", [D, DIN])
    bf_rep_d = din("b_f_rep", [128, H])
    convw_d = din("conv_w_t", [128, 4, 31])
    convp_d = din("conv_par", [128, 3, 4])
    w_pw = din("w_conv_pw", [512, 512])
    w_mkv = din("w_mem_kv", [D, 1024])
    w_out = din("w_out", [D, D])
    ident_d = din("ident", [128, 128], BF16)
    uinc_d = din("uinc", [128, 128])
    maskadd_d = din("maskadd", [128, 8, 128], BF16)
    sel_d = din("sel", [128, NJ, NT])
    out = nc.dram_tensor("out", [NJ * 128, D], F32, kind="ExternalOutput").ap()
    kscr = nc.dram_tensor("kscr", [H, 128, S], BF16, kind="Internal").ap()
    vscr = nc.dram_tensor("vscr", [H, 128, NT, 129], BF16, kind="Internal").ap()
    hscr = nc.dram_tensor("hscr", [128, 16, 1280], BF16, kind="Internal").ap()
    wobf = nc.dram_tensor("wobf", [D, D], BF16, kind="Internal").ap()
    r_wobf = [R() for _ in range(4)]
    hmscr = nc.dram_tensor("hmscr", [128, 16, 256], BF16, kind="Internal").ap()
    qscr = nc.dram_tensor("qscr", [H, 128, 1024], BF16, kind="Internal").ap()
    gscr = nc.dram_tensor("gscr", [H, 128, NJ, 128], BF16, kind="Internal").ap()
    dbg = None
    if debug is not None:
        dbg = nc.dram_tensor("dbg", list(debug[1]), debug[2], kind="ExternalOutput").ap()

    es = ExitStack()
    with es:
        sc = Sched(nc, es)
        cnt = [0]

        def sb(scope, shape, dt, name=None):
            cnt[0] += 1
            return scope.enter_context(nc.sbuf_tensor("%s_%d" % (name or "t", cnt[0]), list(shape), dt))

        def ps(scope, shape, dt, name=None):
            cnt[0] += 1
            return scope.enter_context(nc.psum_tensor("%s_%d" % (name or "p", cnt[0]), list(shape), dt))

        V, A, P, G, SY = "vector", "scalar", "tensor", "gpsimd", "sync"

        def wload(dst, rdst, key, c0, ncols, src=None, rows=D):
            src = w_in if src is None else src
            kt = rows // 128
            half = max(1, kt // 2)
            for a in range(0, kt, half):
                sc.dma(G, key, dst[:, a:a + half, 0:ncols],
                       src[a * 128:(a + half) * 128, c0:c0 + ncols].rearrange("(k p) c -> p k c", p=128),
                       writes=[rdst])

        top = es
        ident = sb(top, [128, 128], BF16, "ident"); r_ident = R()
        uinc = sb(top, [128, 128], F32, "uinc"); r_uinc = R()
        ones_f = sb(top, [128, 128], F32, "onesf"); r_onesf = R()
        g_rep = sb(top, [128, D], F32, "grep"); r_grep = R()
        bf_rep = sb(top, [128, H], F32, "bfrep"); r_bf = R()
        zcol = sb(top, [128, NT, H], F32, "zcol"); r_zcol = R()
        sc.dma(SY, "c0a", ident[:], ident_d, writes=[r_ident])
        sc.dma(SY, "c0b", uinc[:], uinc_d, writes=[r_uinc])
        sc.dma(SY, "c0c", g_rep[:], g_rep_d, writes=[r_grep])
        sc.dma(SY, "c0d", bf_rep[:], bf_rep_d, writes=[r_bf])
        sc.op(V, lambda e: e.memset(ones_f[:], 1.0), writes=[r_onesf])
        cst = sb(top, [128, 4], F32, "cst"); r_cst = R()
        sc.op(V, lambda e: e.memset(cst[:, 0:1], EPS), writes=[r_cst])
        sc.op(V, lambda e: e.memset(cst[:, 1:2], 1.0), writes=[r_cst])

        def rmsnorm_tile(scope_bufs, src_ap, grep, r_g, hdst, r_hdst, key):
            xb, r_xb, ss, r_ss = scope_bufs
            sc.dma(SY, key, xb[:], src_ap, writes=[r_xb])
            sc.op(A, lambda e: e.activation(out=hdst, in_=xb[:], func=AF.Square, accum_out=ss[:, 0:1]),
                  reads=[r_xb], writes=[r_hdst, r_ss])
            sc.op(A, lambda e: e.activation(out=ss[:, 1:2], in_=ss[:, 0:1], func=AF.Sqrt, bias=cst[:, 0:1], scale=1.0 / D),
                  reads=[r_ss, r_cst], writes=[r_ss])
            sc.op(V, lambda e: e.reciprocal(out=ss[:, 2:3], in_=ss[:, 1:2]), reads=[r_ss], writes=[r_ss])
            sc.op(V, lambda e: e.scalar_tensor_tensor(out=hdst, in0=xb[:], scalar=ss[:, 2:3], in1=grep[:],
                                                      op0=ALU.mult, op1=ALU.mult),
                  reads=[r_xb, r_ss, r_g], writes=[r_hdst])

        def transpose_tile(h_ap, r_h, pst, r_pst, dst_fn, r_dst, evac_eng):
            for half in range(2):
                pt, r_pt = pst[half], r_pst[half]
                for kk in range(8):
                    k = half * 8 + kk
                    sc.op(P, lambda e, k=k, kk=kk, pt=pt: e.transpose(out=pt[:, kk, :], in_=h_ap[:, k * 128:(k + 1) * 128],
                                                                       identity=ident[:]),
                          reads=[r_h, r_ident], writes=[r_pt], sig=(kk == 7))
                dst = dst_fn(half * 8, half * 8 + 8)
                if evac_eng == A:
                    sc.op(A, lambda e, pt=pt, dst=dst: e.copy(out=dst, in_=pt[:]), reads=[r_pt], writes=[r_dst])
                else:
                    sc.op(V, lambda e, pt=pt, dst=dst: e.tensor_copy(out=dst, in_=pt[:]), reads=[r_pt], writes=[r_dst])

        r_kchunk = [R() for _ in range(16)]
        r_vchunk = [R() for _ in range(16)]
        with ExitStack() as p1:
            wk = sb(p1, [128, 16, 1024], BF16, "wk"); r_wk = R()
            wv = sb(p1, [128, 16, 1032], BF16, "wv"); r_wv = R()
            r_wkh = [R() for _ in range(H)]
            r_wvg = [R() for _ in range(3)]
            NXB = 5
            xbs = [(sb(p1, [128, D], F32, "xb"), R(), sb(p1, [128, 4], F32, "ss"), R()) for _ in range(NXB)]
            hbs = [(sb(p1, [128, D], BF16, "hb"), R()) for _ in range(4)]
            hTs = [(sb(p1, [128, 16, 512], BF16, "hT"), R()) for _ in range(2)]
            ksts = [(sb(p1, [128, H, 512], BF16, "kst"), R()) for _ in range(1)]
            vsts = [(sb(p1, [128, H, 4, 129], BF16, "vst"), R()) for _ in range(2)]
            for vv in range(2):
                sc.op(G, lambda e, vv=vv: e.memset(vsts[vv][0][:], 1.0), writes=[vsts[vv][1]])
            pT = [ps(p1, [128, 8, 128], BF16, "pT") for _ in range(2)]; r_pT = [R(), R()]
            pmm = [ps(p1, [128, 512], F32, "pmm") for _ in range(4)]; r_pmm = [R() for _ in range(4)]
            pz = ps(p1, [128, 512], F32, "pz"); r_pz = R()
            nchunks = 16 if stage >= 2 else 1
            cnt1 = {"mi": 0}

            tcn = {"t": 0}
            slot_of = {}
            r_hscr = [R() for _ in range(10)]
            hsos = [(sb(p1, [128, 16, 128], BF16, "hso"), R()) for _ in range(2)]

            def rms(g, i):
                t = tcn["t"]
                tcn["t"] += 1
                slot_of[(g, i)] = t
                ti = g * 4 + i
                hb, r_hb = hbs[t % 4]
                rmsnorm_tile(xbs[t % NXB], x_all[ti * 128:(ti + 1) * 128, :], g_rep, r_grep, hb[:], r_hb, "xb%d" % (t % NXB))

            def tr(g, i):
                t = slot_of[(g, i)]
                hb, r_hb = hbs[t % 4]
                hT, r_hT = hTs[g % 2]
                transpose_tile(hb, r_hb, pT, r_pT, lambda k0, k1, i=i, hT=hT: hT[:, k0:k1, i * 128:(i + 1) * 128],
                               r_hT, A if (t % 2 == 0) else V)

            def rms_own(ot):
                t = tcn["t"]
                tcn["t"] += 1
                slot_of[("own", ot)] = t
                hb, r_hb = hbs[t % 4]
                src = x_own[ot * 128:(ot + 1) * 128, :] if ot < 8 else x_halo[(ot - 8) * 128:(ot - 7) * 128, :]
                rmsnorm_tile(xbs[t % NXB], src, g_rep, r_grep, hb[:], r_hb, "xb%d" % (t % NXB))

            mg_rep1 = sb(p1, [128, D], F32, "mgrep1"); r_mg1 = R()
            sc.dma(SY, "c3", mg_rep1[:], mg_rep_d, writes=[r_mg1])
            r_hmscr = [R(), R()]

            def rms_mem(mt):
                t = tcn["t"]
                tcn["t"] += 1
                slot_of[("mem", mt)] = t
                hb, r_hb = hbs[t % 4]
                rmsnorm_tile(xbs[t % NXB], mem[mt * 128:(mt + 1) * 128, :], mg_rep1, r_mg1, hb[:], r_hb, "xb%d" % (t % NXB))

            def tr_mem(mt):
                t = slot_of[("mem", mt)]
                hb, r_hb = hbs[t % 4]
                hso, r_hso = hsos[mt % 2]
                transpose_tile(hb, r_hb, pT, r_pT, lambda k0, k1, hso=hso: hso[:, k0:k1, :], r_hso, A if (t % 2 == 0) else V)
                sc.dma(SY, "hso%d" % (mt % 2), hmscr[:, :, mt * 128:(mt + 1) * 128], hso[:], reads=[r_hso], writes=[r_hmscr[mt]])

            def tr_own(ot):
                t = slot_of[("own", ot)]
                hb, r_hb = hbs[t % 4]
                hso, r_hso = hsos[ot % 2]
                transpose_tile(hb, r_hb, pT, r_pT, lambda k0, k1, hso=hso: hso[:, k0:k1, :], r_hso, A if (t % 2 == 0) else V)
                sc.dma(SY, "hso%d" % (ot % 2), hscr[:, :, ot * 128:(ot + 1) * 128], hso[:], reads=[r_hso], writes=[r_hscr[ot]])

            for i in range(4):
                rms(0, i)
            first_x = [xbs[i][1].w for i in range(4)]
            for hh in range(H):
                sc.dma(G, "wk%d" % hh, wk[:, :, hh * 128:(hh + 1) * 128],
                       w_in[:, C_K + hh * 128:C_K + (hh + 1) * 128].rearrange("(k p) c -> p k c", p=128),
                       after=(first_x if hh >= 1 else []), writes=[r_wkh[hh]])
            for cg in range(2):
                sc.dma(G, "wv%d" % cg, wv[:, :, cg * 512:(cg + 1) * 512],
                       w_in[:, C_V + cg * 512:C_V + (cg + 1) * 512].rearrange("(k p) c -> p k c", p=128), after=first_x, writes=[r_wvg[cg]])
            sc.dma(G, "wv2", wv[:, :, 1024:1032], w_in[:, C_F:C_F + 8].rearrange("(k p) c -> p k c", p=128), after=first_x, writes=[r_wvg[2]])
            for i in range(4):
                tr(0, i)
            for g in range(nchunks):
                hT, r_hT = hTs[g % 2]
                kst, r_kst = ksts[0]
                vst, r_vst = vsts[g % 2]
                for hh in range(H):
                    mi = cnt1["mi"]
                    pm, r_pm = pmm[mi % 4], r_pmm[mi % 4]
                    for k in range(16):
                        sc.op(P, lambda e, k=k, hh=hh, pm=pm, hT=hT: e.matmul(out=pm[:], lhsT=wk[:, k, hh * 128:(hh + 1) * 128],
                                                                               rhs=hT[:, k, :], start=(k == 0), stop=(k == 15)),
                              reads=[r_wkh[hh], r_hT], writes=[r_pm], sig=(k == 15))
                    if mi % 2 == 0:
                        sc.op(A, lambda e, pm=pm, kst=kst, hh=hh: e.copy(out=kst[:, hh, :], in_=pm[:]), reads=[r_pm], writes=[r_kst])
                    else:
                        sc.op(V, lambda e, pm=pm, kst=kst, hh=hh: e.tensor_copy(out=kst[:, hh, :], in_=pm[:]), reads=[r_pm], writes=[r_kst])
                    cnt1["mi"] += 1
                    if nchunks == 16 and 3 <= g < 13:
                        if hh == 0:
                            rms_own(g - 3)
                        if hh == 2:
                            tr_own(g - 3)
                    if nchunks == 16 and 4 <= g < 8 and hh == 4:
                        wq = g - 4
                        sc.dma(G, "wocast%d" % wq, wobf[wq * 512:(wq + 1) * 512, :], w_out[wq * 512:(wq + 1) * 512, :], writes=[r_wobf[wq]])
                    if nchunks == 16 and 13 <= g < 15:
                        if hh == 0:
                            rms_mem(g - 13)
                        if hh == 2:
                            tr_mem(g - 13)
                    if hh % 2 == 1 and g + 1 < nchunks:
                        rms(g + 1, hh // 2)
                sc.dma(G, "kst0", kscr[:, :, g * 512:(g + 1) * 512].rearrange("h d t -> d h t"), kst[:],
                       reads=[r_kst], writes=[r_kchunk[g]])
                for i in range(4):
                    for cg in range(2):
                        mi = cnt1["mi"]
                        pm, r_pm = pmm[mi % 4], r_pmm[mi % 4]
                        for k in range(16):
                            sc.op(P, lambda e, k=k, i=i, cg=cg, pm=pm, hT=hT: e.matmul(out=pm[:], lhsT=hT[:, k, i * 128:(i + 1) * 128],
                                                                                       rhs=wv[:, k, cg * 512:(cg + 1) * 512],
                                                                                       start=(k == 0), stop=(k == 15)),
                                  reads=[r_wvg[cg], r_hT], writes=[r_pm], sig=(k == 15))
                        if mi % 2 == 0:
                            sc.op(A, lambda e, pm=pm, vst=vst, i=i, cg=cg: e.copy(out=vst[:, cg * 4:(cg + 1) * 4, i, 0:128],
                                                                                  in_=pm[:].rearrange("p (h d) -> p h d", d=128)),
                                  reads=[r_pm], writes=[r_vst])
                        else:
                            sc.op(V, lambda e, pm=pm, vst=vst, i=i, cg=cg: e.tensor_copy(out=vst[:, cg * 4:(cg + 1) * 4, i, 0:128],
                                                                                         in_=pm[:].rearrange("p (h d) -> p h d", d=128)),
                                  reads=[r_pm], writes=[r_vst])
                        cnt1["mi"] += 1
                    for k in range(16):
                        sc.op(P, lambda e, k=k, i=i, hT=hT: e.matmul(out=pz[:, i * 8:(i + 1) * 8], lhsT=hT[:, k, i * 128:(i + 1) * 128],
                                                                       rhs=wv[:, k, 1024:1032], start=(k == 0), stop=(k == 15)),
                              reads=[r_wvg[2], r_hT], writes=[r_pz], sig=(k == 15))
                    if g + 1 < nchunks:
                        tr(g + 1, i)
                sc.op(V, lambda e, g=g: e.tensor_tensor(out=zcol[:, g * 4:(g + 1) * 4, :],
                                                        in0=pz[:, 0:32].rearrange("p (i h) -> p i h", h=H),
                                                        in1=bf_rep[:, None, :].to_broadcast([128, 4, H]), op=ALU.add),
                      reads=[r_pz, r_bf], writes=[r_zcol])
                sc.dma(G, "vst%d" % (g % 2), vscr[:, :, g * 4:(g + 1) * 4, :].rearrange("h p i c -> p h (i c)"), vst[:].rearrange("p h i c -> p h (i c)"),
                       reads=[r_vst], writes=[r_vchunk[g]])
            if debug is not None and debug[0] == "p1":
                lg = (nchunks - 1) % 2
                sc.dma(G, "dbg", dbg[:, 0:8192], hTs[lg][0][:].rearrange("p k t -> p (k t)"), reads=[hTs[lg][1]])
                sc.dma(G, "dbg", dbg[:, 16384:16384 + 512], zcol[:].rearrange("p t h -> p (t h)"), reads=[r_zcol])

        def barrier():
            toks = sc.final_tokens()
            for e in sc.E.values():
                e.wait(toks)

        def evac_copy(i, dst, src, r_src, r_dst, scale=None):
            if i % 2 == 0:
                if scale is None:
                    sc.op(A, lambda e: e.copy(out=dst, in_=src), reads=[r_src], writes=[r_dst])
                else:
                    sc.op(A, lambda e: e.activation(out=dst, in_=src, func=AF.Copy, scale=scale), reads=[r_src], writes=[r_dst])
            else:
                if scale is None:
                    sc.op(V, lambda e: e.tensor_copy(out=dst, in_=src), reads=[r_src], writes=[r_dst])
                else:
                    sc.op(V, lambda e: e.tensor_scalar_mul(out=dst, in0=src, scalar1=scale), reads=[r_src], writes=[r_dst])

        barrier()
        if stage >= 3:
          with ExitStack() as L1:
            r_qscr = R(); r_gscr = R()
            ycT = sb(L1, [128, 4, 1024], BF16, "ycT"); r_ycT = R()
            ymT = sb(L1, [128, 4, 1024], BF16, "ymT"); r_ymT = R()
            biasall = sb(L1, [128, 288, H], F32, "biasall"); r_biasall = R()
            maskadd = sb(L1, [128, 8, 128], BF16, "maskadd"); r_mask = R()
            sc.dma(SY, "c1a", maskadd[:], maskadd_d, writes=[r_mask])
            with ExitStack() as mid:
                mqT = sb(mid, [128, 4, 1024], BF16, "mqT"); r_mqT = R()
                gmem = sb(mid, [128, NJ, 512], BF16, "gmem"); r_gmem = R()
                mkT = sb(mid, [128, 4, 256], BF16, "mkT"); r_mkT = R()
                mv = sb(mid, [128, 2, 4, 129], BF16, "mv"); r_mv = R()
                mid2 = ExitStack()
                uext = sb(mid2, [128, 4, NJ, 160], F32, "uext"); r_uext = R()
                gconv = sb(mid2, [128, 4, 1024], BF16, "gconv"); r_gconv = R()
                with ExitStack() as own:
                    hTo = sb(own, [128, 16, 1024], BF16, "hTo"); r_hToA = R(); r_hToB = R()
                    qT = sb(own, [128, H, 1024], BF16, "qT"); r_qT = R()
                    gfox = sb(own, [128, NJ, 1024], BF16, "gfox"); r_gfox = R()
                    hTh = sb(own, [128, 16, 256], BF16, "hTh"); r_hTh = R()
                    pT = [ps(own, [128, 8, 128], BF16, "pT") for _ in range(2)]; r_pT = [R(), R()]
                    pmm = [ps(own, [128, 512], F32, "pmm") for _ in range(4)]; r_pmm = [R() for _ in range(4)]
                    sc.dma(SY, "hToA", hTo[:, :, 0:512], hscr[:, :, 0:512], reads=r_hscr[0:4], writes=[r_hToA])
                    sc.dma(SY, "hToB", hTo[:, :, 512:1024], hscr[:, :, 512:1024], reads=r_hscr[4:8], writes=[r_hToB])
                    sc.dma(SY, "hTh", hTh[:], hscr[:, :, 1024:1280], reads=r_hscr[8:10], writes=[r_hTh])
                    with ExitStack() as ob_:
                        GW = 256
                        wbs = [(sb(ob_, [128, 16, GW], BF16, "wb"), R()) for _ in range(2)]
                        convw = sb(ob_, [128, 4, 31], F32, "convw"); r_convw = R()
                        convp = sb(ob_, [128, 3, 4], F32, "convp"); r_convp = R()
                        acc = sb(ob_, [128, 4, 1024], F32, "acc"); r_acc = [R() for _ in range(4)]
                        u2T = sb(ob_, [128, 4, 1024], BF16, "u2T"); r_u2T = R()
                        wpw = sb(ob_, [128, 4, 512], BF16, "wpw"); r_wpw = R()
                        sc.dma(SY, "c2a", convw[:], convw_d, writes=[r_convw])
                        sc.dma(SY, "c2b", convp[:], convp_d, writes=[r_convp])
                        accf = acc[:].rearrange("p c t -> p (c t)")
                        lcol = accf[:, 0:512]; r_lcol = R()
                        sA = accf[:, 512:1024].rearrange("p (t h) -> p t h", h=H); r_sA = R()
                        sB = accf[:, 1024:1536].rearrange("p (t h) -> p t h", h=H); r_sB = R()
                        tmpe = accf[:, 1536:2048].rearrange("p (h t) -> p h t", t=NT); r_tmpe = R()
                        negc = accf[:, 2048:2560].rearrange("p (t h) -> p t h", h=H); r_negc = R()
                        tsb = accf[:, 2560:3072].rearrange("p (t h) -> p t h", h=H); r_tsb = R()
                        sel = accf[:, 3072:3584].rearrange("p (j t) -> p j t", t=NT); r_sel = R()
                        eown = accf[:, 3584:3648].rearrange("p (j h) -> p j h", h=H); r_eown = R()
                        sc.dma(SY, "c1b", sel, sel_d, writes=[r_sel])
                        pW = pmm[2]; r_pW = r_pmm[2]
                        pTt = pmm[3]; r_pTt = r_pmm[3]
                        sc.op(A, lambda e: e.activation(out=lcol, in_=zcol[:].rearrange("p t h -> p (t h)"), func=AF.Exp, scale=-1.0),
                              reads=[r_zcol], writes=[r_lcol])
                        sc.op(A, lambda e: e.activation(out=lcol, in_=lcol, func=AF.Ln, bias=cst[:, 1:2], scale=1.0),
                              reads=[r_lcol, r_cst], writes=[r_lcol])
                        sc.op(P, lambda e: e.matmul(out=pW[:], lhsT=uinc[:], rhs=lcol, start=True, stop=True), reads=[r_uinc, r_lcol], writes=[r_pW])
                        sc.op(P, lambda e: e.matmul(out=pTt[:], lhsT=ones_f[:], rhs=lcol, start=True, stop=True), reads=[r_onesf, r_lcol], writes=[r_pTt])
                        sc.op(V, lambda e: e.tensor_copy(out=tsb.rearrange("p t h -> p (t h)"), in_=pTt[:]), reads=[r_pTt], writes=[r_tsb])
                        cur, r_cur = tsb, r_tsb
                        bufs = [(sA, r_sA), (sB, r_sB)]
                        for si, s_ in enumerate((1, 2, 4, 8, 16, 32)):
                            nxt, r_nxt = bufs[si % 2]
                            sc.op(V, lambda e, nxt=nxt, cur=cur, s_=s_: e.tensor_copy(out=nxt[:, 0:s_, :], in_=cur[:, 0:s_, :]), reads=[r_cur], writes=[r_nxt])
                            sc.op(V, lambda e, nxt=nxt, cur=cur, s_=s_: e.tensor_tensor(out=nxt[:, s_:NT, :], in0=cur[:, s_:NT, :], in1=cur[:, 0:NT - s_, :], op=ALU.add),
                                  reads=[r_cur], writes=[r_nxt])
                            cur, r_cur = nxt, r_nxt
                        einc, r_einc = cur, r_cur
                        sc.op(V, lambda e: e.tensor_tensor(out=negc.rearrange("p t h -> p (t h)"), in0=pW[:], in1=einc.rearrange("p t h -> p (t h)"), op=ALU.add),
                              reads=[r_pW, r_einc], writes=[r_negc])
                        sc.op(V, lambda e: e.tensor_tensor(out=negc, in0=negc, in1=tsb, op=ALU.subtract), reads=[r_negc, r_tsb], writes=[r_negc])
                        for j in range(NJ):
                            sc.op(V, lambda e, j=j: e.tensor_tensor(out=tmpe, in0=tsb.rearrange("p t h -> p h t"),
                                                                     in1=sel[:, j:j + 1, :].to_broadcast([128, H, NT]), op=ALU.mult),
                                  reads=[r_tsb, r_sel], writes=[r_tmpe])
                            sc.op(V, lambda e, j=j: e.reduce_sum(out=eown[:, j, :], in_=tmpe, axis=AX.X), reads=[r_tmpe], writes=[r_eown])
                        for j in range(NJ):
                            nk = 8 * j + 8
                            off = 4 * j * (j + 1)
                            sc.op(V, lambda e, j=j, nk=nk, off=off: e.tensor_tensor(out=biasall[:, off:off + nk, :], in0=negc[:, 0:nk, :],
                                                                                     in1=eown[:, j:j + 1, :].to_broadcast([128, nk, H]), op=ALU.subtract),
                                  reads=[r_negc, r_eown], writes=[r_biasall])
                        st = {"wi": 0, "mi": 0}

                        wb_rs = [[R(), R()], [R(), R()]]

                        def getw(c0, ncols):
                            si = st["wi"] % 2
                            wb = wbs[si][0]
                            for ct in range(ncols // 128):
                                sc.dma(G, "wb%d_%d" % (si, ct), wb[:, :, ct * 128:(ct + 1) * 128],
                                       w_in[:, c0 + ct * 128:c0 + (ct + 1) * 128].rearrange("(k p) c -> p k c", p=128),
                                       writes=[wb_rs[si][ct]])
                            st["wi"] += 1
                            return wb, wb_rs[si]

                        def fm_group(c0, evac, halo=False):
                            wb, r_wbs = getw(c0, GW)
                            for ct in range(GW // 128):
                                r_wb = r_wbs[ct]
                                chunks = [(hTo, r_hToA, 0, 512, 0), (hTo, r_hToB, 512, 512, 1)]
                                if halo:
                                    chunks.append((hTh, r_hTh, 0, 256, 2))
                                for (src, r_src, o, n, qc) in chunks:
                                    pm, r_pm = pmm[st["mi"] % 4], r_pmm[st["mi"] % 4]
                                    for k in range(16):
                                        sc.op(P, lambda e, k=k, ct=ct, pm=pm, src=src, o=o, n=n, wb=wb: e.matmul(
                                            out=pm[:, 0:n], lhsT=wb[:, k, ct * 128:(ct + 1) * 128], rhs=src[:, k, o:o + n],
                                            start=(k == 0), stop=(k == 15)), reads=[r_wb, r_src], writes=[r_pm], sig=(k == 15))
                                    evac(ct, qc, pm, r_pm, st["mi"])
                                    st["mi"] += 1

                        def tm_group(c0, ncols, evac):
                            wb, r_wbs = getw(c0, ncols)
                            for j in range(NJ):
                                pm, r_pm = pmm[st["mi"] % 4], r_pmm[st["mi"] % 4]
                                r_h = r_hToA if j < 4 else r_hToB
                                for k in range(16):
                                    sc.op(P, lambda e, k=k, j=j, pm=pm, wb=wb: e.matmul(
                                        out=pm[:, 0:ncols], lhsT=hTo[:, k, j * 128:(j + 1) * 128], rhs=wb[:, k, 0:ncols],
                                        start=(k == 0), stop=(k == 15)), reads=[r_wbs[0], r_wbs[1], r_h], writes=[r_pm], sig=(k == 15))
                                evac(j, pm, r_pm, st["mi"])
                                st["mi"] += 1

                        def udst(ct, qc):
                            if qc < 2:
                                return uext[:, ct, 4 * qc:4 * qc + 4, 32:160], (lambda pm: pm[:, 0:512].rearrange("p (j t) -> p j t", t=128))
                            return uext[:, ct, :, 0:32], (lambda pm: pm[:, 0:256].rearrange("p (j t) -> p j t", t=32))

                        for gi in range(2):
                            def ev_b(ct, qc, pm, r_pm, mi, gi=gi):
                                d, v = udst(gi * 2 + ct, qc)
                                sc.op(A, lambda e: e.activation(out=d, in_=v(pm), func=AF.Sigmoid), reads=[r_pm], writes=[r_uext])
                            fm_group(C_B + gi * GW, ev_b, halo=True)
                        for gi in range(2):
                            def ev_a(ct, qc, pm, r_pm, mi, gi=gi):
                                d, v = udst(gi * 2 + ct, qc)
                                sc.op(V, lambda e: e.tensor_tensor(out=d, in0=v(pm), in1=d, op=ALU.mult), reads=[r_pm, r_uext], writes=[r_uext])
                            fm_group(C_A + gi * GW, ev_a, halo=True)
                        uflat = uext[:].rearrange("p c j t -> p (c j t)")
                        for ct in range(4):
                            a_ct = acc[:, ct, :].rearrange("p (j t) -> p j t", t=128)
                            sc.op(V, lambda e, ct=ct, a_ct=a_ct: e.tensor_scalar(out=a_ct, in0=uext[:, ct, :, 2:130], scalar1=convw[:, ct, 0:1],
                                                                                 scalar2=convp[:, 0, ct:ct + 1], op0=ALU.mult, op1=ALU.add),
                                  reads=[r_uext, r_convw, r_convp, r_biasall], writes=[r_acc[ct]])
                            for k in range(1, 31):
                                sc.op(V, lambda e, ct=ct, k=k, a_ct=a_ct: e.scalar_tensor_tensor(out=a_ct, in0=uext[:, ct, :, 2 + k:130 + k],
                                                                                                  scalar=convw[:, ct, k:k + 1], in1=a_ct,
                                                                                                  op0=ALU.mult, op1=ALU.add),
                                      reads=[r_uext, r_convw, r_acc[ct]], writes=[r_acc[ct]])
                        for gi in range(2):
                            def ev_g(ct, qc, pm, r_pm, mi, gi=gi):
                                sc.op(A, lambda e: e.activation(out=gconv[:, gi * 2 + ct, qc * 512:(qc + 1) * 512], in_=pm[:], func=AF.Silu),
                                      reads=[r_pm], writes=[r_gconv])
                            fm_group(C_G + gi * GW, ev_g)
                        for gi in range(4):
                            def ev_q(ct, qc, pm, r_pm, mi, gi=gi):
                                evac_copy(0, qT[:, gi * 2 + ct, qc * 512:(qc + 1) * 512], pm[:], r_pm, r_qT, scale=SCALE)
                            fm_group(C_Q + gi * GW, ev_q)
                        for gi in range(2):
                            def ev_mq(ct, qc, pm, r_pm, mi, gi=gi):
                                evac_copy(0, mqT[:, gi * 2 + ct, qc * 512:(qc + 1) * 512], pm[:], r_pm, r_mqT, scale=SCALE)
                            fm_group(C_MQ + gi * GW, ev_mq)
                        for gi in range(4):
                            def ev_fg(j, pm, r_pm, mi, gi=gi):
                                sc.op(A, lambda e: e.activation(out=gfox[:, j, gi * GW:(gi + 1) * GW], in_=pm[:, 0:GW], func=AF.Silu),
                                      reads=[r_pm], writes=[r_gfox])
                            tm_group(C_FG + gi * GW, GW, ev_fg)
                        for ct in range(4):
                            sc.op(A, lambda e, ct=ct: e.activation(out=uflat[:, ct * 1024:(ct + 1) * 1024], in_=acc[:, ct, :], func=AF.Square),
                                  reads=[r_acc[ct]], writes=[r_uext])
                        for gi in range(2):
                            def ev_mg(j, pm, r_pm, mi, gi=gi):
                                sc.op(A, lambda e: e.activation(out=gmem[:, j, gi * GW:(gi + 1) * GW], in_=pm[:, 0:GW], func=AF.Silu),
                                      reads=[r_pm], writes=[r_gmem])
                            tm_group(C_MG + gi * GW, GW, ev_mg)

                        sc.dma(SY, "qspill", qscr.rearrange("h d t -> d h t"), qT[:], reads=[r_qT], writes=[r_qscr])
                        sc.dma(SY, "gspill", gscr.rearrange("h p j d -> p j h d"), gfox[:].rearrange("p j (h d) -> p j h d", d=128),
                               reads=[r_gfox], writes=[r_gscr])
                        wmkv, r_wmkv = hTo, R()
                        hmT, r_hmT = wbs[0][0], R()
                        r_wmv = R()
                        hto_readers = [t for r_ in (r_hToA, r_hToB) for t in ([r_.w] + list(r_.rs.values()))]
                        sc.dma(G, "wmk", wmkv[:, :, 0:512], w_mkv[:, 0:512].rearrange("(k p) c -> p k c", p=128), writes=[r_wmkv, r_hToA, r_hToB])
                        sc.dma(G, "wmv", wmkv[:, :, 512:1024], w_mkv[:, 512:1024].rearrange("(k p) c -> p k c", p=128), writes=[r_wmv], after=hto_readers)
                        sc.dma(SY, "hmT", hmT[:], hmscr, reads=r_hmscr, writes=[r_hmT, wb_rs[0][0], wb_rs[0][1]])
                        sc.op(G, lambda e: e.memset(mv[:], 1.0), writes=[r_mv])
                        wload(wpw, r_wpw, "wpw", 0, 512, src=w_pw, rows=512)
                        mr = hTh[:].rearrange("p k t -> p (k t)").bitcast(F32)
                        mean = mr[:, 0:1024]
                        rstd = mr[:, 1024:2048]
                        r_mean = r_hTh
                        r_rstd = r_hTh
                        for qc in range(2):
                            p1_, rp1 = pmm[qc], r_pmm[qc]
                            p2_, rp2 = pmm[2 + qc], r_pmm[2 + qc]
                            for ct in range(4):
                                sc.op(P, lambda e, ct=ct, qc=qc, p1_=p1_: e.matmul(out=p1_[:], lhsT=ones_f[:], rhs=acc[:, ct, qc * 512:(qc + 1) * 512],
                                                                                   start=(ct == 0), stop=(ct == 3)), reads=[r_onesf, r_acc[ct]], writes=[rp1], sig=(ct == 3))
                            for ct in range(4):
                                sc.op(P, lambda e, ct=ct, qc=qc, p2_=p2_: e.matmul(out=p2_[:], lhsT=ones_f[:], rhs=uflat[:, ct * 1024 + qc * 512:ct * 1024 + (qc + 1) * 512],
                                                                                   start=(ct == 0), stop=(ct == 3)), reads=[r_onesf, r_uext], writes=[rp2], sig=(ct == 3))
                            sl = slice(qc * 512, (qc + 1) * 512)
                            sc.op(V, lambda e, p1_=p1_, sl=sl: e.tensor_scalar_mul(out=mean[:, sl], in0=p1_[:], scalar1=1.0 / 512), reads=[rp1], writes=[r_mean])
                            sc.op(V, lambda e, sl=sl: e.tensor_tensor(out=rstd[:, sl], in0=mean[:, sl], in1=mean[:, sl], op=ALU.mult), reads=[r_mean], writes=[r_rstd])
                            sc.op(V, lambda e, p2_=p2_, sl=sl: e.scalar_tensor_tensor(out=rstd[:, sl], in0=p2_[:], scalar=1.0 / 512, in1=rstd[:, sl],
                                                                                      op0=ALU.mult, op1=ALU.subtract), reads=[rp2, r_rstd], writes=[r_rstd])
                            sc.op(A, lambda e, sl=sl: e.activation(out=rstd[:, sl], in_=rstd[:, sl], func=AF.Sqrt, bias=cst[:, 0:1], scale=1.0),
                                  reads=[r_rstd, r_cst], writes=[r_rstd])
                            sc.op(V, lambda e, sl=sl: e.reciprocal(out=rstd[:, sl], in_=rstd[:, sl]), reads=[r_rstd], writes=[r_rstd])
                        for ct in range(4):
                            sc.op(V, lambda e, ct=ct: e.tensor_tensor(out=acc[:, ct, :], in0=acc[:, ct, :], in1=mean, op=ALU.subtract),
                                  reads=[r_acc[ct], r_mean], writes=[r_acc[ct]])
                            sc.op(V, lambda e, ct=ct: e.tensor_tensor(out=acc[:, ct, :], in0=acc[:, ct, :], in1=rstd, op=ALU.mult),
                                  reads=[r_acc[ct], r_rstd], writes=[r_acc[ct]])
                            sc.op(A, lambda e, ct=ct: e.activation(out=u2T[:, ct, :], in_=acc[:, ct, :], func=AF.Silu, bias=convp[:, 2, ct:ct + 1],
                                                                  scale=convp[:, 1, ct:ct + 1]), reads=[r_acc[ct], r_convp], writes=[r_u2T])
                        mi2 = 0
                        for co in range(4):
                            for qc in range(2):
                                pm, r_pm = pmm[mi2 % 4], r_pmm[mi2 % 4]
                                for ci in range(4):
                                    sc.op(P, lambda e, ci=ci, co=co, qc=qc, pm=pm: e.matmul(out=pm[:], lhsT=wpw[:, ci, co * 128:(co + 1) * 128],
                                                                                            rhs=u2T[:, ci, qc * 512:(qc + 1) * 512], start=(ci == 0), stop=(ci == 3)),
                                          reads=[r_wpw, r_u2T], writes=[r_pm], sig=(ci == 3))
                                sc.op(V, lambda e, co=co, qc=qc, pm=pm: e.tensor_tensor(out=ycT[:, co, qc * 512:(qc + 1) * 512], in0=pm[:],
                                                                                        in1=gconv[:, co, qc * 512:(qc + 1) * 512], op=ALU.mult),
                                      reads=[r_pm, r_gconv], writes=[r_ycT])
                                mi2 += 1
                        for hh in range(4):
                            pm, r_pm = pmm[hh % 4], r_pmm[hh % 4]
                            for k in range(16):
                                sc.op(P, lambda e, k=k, hh=hh, pm=pm: e.matmul(out=pm[:, 0:256], lhsT=wmkv[:, k, hh * 128:(hh + 1) * 128], rhs=hmT[:, k, :],
                                                                               start=(k == 0), stop=(k == 15)), reads=[r_wmkv, r_hmT], writes=[r_pm], sig=(k == 15))
                            evac_copy(hh, mkT[:, hh, :], pm[:, 0:256], r_pm, r_mkT)
                        for mt in range(2):
                            pm, r_pm = pmm[mt % 4], r_pmm[mt % 4]
                            for k in range(16):
                                sc.op(P, lambda e, k=k, mt=mt, pm=pm: e.matmul(out=pm[:], lhsT=hmT[:, k, mt * 128:(mt + 1) * 128], rhs=wmkv[:, k, 512:1024],
                                                                               start=(k == 0), stop=(k == 15)), reads=[r_wmv, r_hmT], writes=[r_pm], sig=(k == 15))
                            evac_copy(mt, mv[:, mt, :, 0:128], pm[:].rearrange("p (h d) -> p h d", d=128), r_pm, r_mv)
                        dead_acc = [t for ct in range(4) for t in ([r_acc[ct].w] + list(r_acc[ct].rs.values()))]
                        dead_u2T = [r_u2T.w] + list(r_u2T.rs.values())
                        ymem = accf[:, 0:2048].bitcast(BF16).rearrange("p (j c) -> p j c", c=512); r_ymem = R()
                        u2f = u2T[:].rearrange("p c t -> p (c t)")
                        pTm = [u2f[:, i * 512:(i + 1) * 512] for i in range(4)]; r_pTm = [R() for _ in range(4)]
                        rec = sb(ob_, [128, 4], F32, "recm"); r_rec = R()
                        units = [(hh, qc) for hh in range(4) for qc in range(2)]

                        def mem_S(u):
                            hh, qc = units[u]
                            for mt in range(2):
                                ps_, rps = pmm[mt], r_pmm[mt]
                                tm, rtm = pTm[(u % 2) * 2 + mt], r_pTm[(u % 2) * 2 + mt]
                                sc.op(P, lambda e, ps_=ps_, mt=mt: e.matmul(out=ps_[:], lhsT=mkT[:, hh, mt * 128:(mt + 1) * 128],
                                                                            rhs=mqT[:, hh, qc * 512:(qc + 1) * 512], start=True, stop=True),
                                      reads=[r_mkT, r_mqT], writes=[rps])
                                sc.op(A, lambda e, ps_=ps_, tm=tm: e.activation(out=tm, in_=ps_[:], func=AF.Exp), reads=[rps], writes=[rtm],
                                      after=(dead_u2T if u < 2 else ()))

                        def mem_PV(u, ai0):
                            hh, qc = units[u]
                            for jq in range(4):
                                j = qc * 4 + jq
                                ai = ai0 + jq
                                pa, r_pa = pmm[2 + ai % 2], r_pmm[2 + ai % 2]
                                for mt in range(2):
                                    tm, rtm = pTm[(u % 2) * 2 + mt], r_pTm[(u % 2) * 2 + mt]
                                    sc.op(P, lambda e, mt=mt, tm=tm, pa=pa, jq=jq: e.matmul(out=pa[:, 0:129], lhsT=tm[:, jq * 128:(jq + 1) * 128],
                                                                                     rhs=mv[:, mt, hh, :], start=(mt == 0), stop=(mt == 1)),
                                          reads=[rtm, r_mv], writes=[r_pa], sig=(mt == 1))
                                rc = rec[:, (ai % 4):(ai % 4) + 1]
                                sc.op(V, lambda e, pa=pa, rc=rc: e.reciprocal(out=rc, in_=pa[:, 128:129]), reads=[r_pa], writes=[r_rec])
                                sc.op(V, lambda e, pa=pa, rc=rc, j=j: e.scalar_tensor_tensor(out=ymem[:, j, hh * 128:(hh + 1) * 128], in0=pa[:, 0:128],
                                                                                             scalar=rc, in1=gmem[:, j, hh * 128:(hh + 1) * 128],
                                                                                             op0=ALU.mult, op1=ALU.mult),
                                      reads=[r_pa, r_rec, r_gmem], writes=[r_ymem], after=(dead_acc if (u == 0 and jq == 0) else ()))

                        mem_S(0)
                        for u in range(len(units)):
                            if u + 1 < len(units):
                                mem_S(u + 1)
                            mem_PV(u, u * 4)
                        for j in range(NJ):
                            pt, r_pt = pT[j % 2], r_pT[j % 2]
                            for hh in range(4):
                                sc.op(P, lambda e, j=j, hh=hh, pt=pt: e.transpose(out=pt[:, hh, :], in_=ymem[:, j, hh * 128:(hh + 1) * 128], identity=ident[:]),
                                      reads=[r_ymem, r_ident], writes=[r_pt], sig=(hh == 3))
                            evac_copy(j, ymT[:, :, j * 128:(j + 1) * 128], pt[:, 0:4, :], r_pt, r_ymT)
                        barrier()
                mid2.close()
            with ExitStack() as atO:
                yfT = sb(atO, [128, H, 1024], BF16, "yfT"); r_yfT = R()
                woA = sb(atO, [128, 8, D], BF16, "woA"); r_woA = R()
                woB = sb(atO, [128, 8, D], BF16, "woB"); r_woB = R()
                with ExitStack() as at:
                    KT = [sb(at, [128, S], BF16, "KT") for _ in range(2)]; r_KT = [[R() for _ in range(4)] for _ in range(2)]
                    Vh = [sb(at, [128, NT, 129], BF16, "Vh") for _ in range(2)]; r_Vh = [[R(), R()] for _ in range(2)]
                    Sb = [sb(at, [128, 4, 128], F32, "Sb") for _ in range(3)]; r_Sb = [R() for _ in range(3)]
                    pTb = [sb(at, [128, 4, 128], BF16, "pTb") for _ in range(4)]; r_pTb = [R() for _ in range(4)]
                    yfox = [sb(at, [128, 128], BF16, "yfox") for _ in range(2)]; r_yfox = [R(), R()]
                    rec = sb(at, [128, 4], F32, "rec"); r_rec = R()
                    qh = [sb(at, [128, 1024], BF16, "qh") for _ in range(2)]; r_qh = [R(), R()]
                    mbias = [sb(at, [128, 4, 128], F32, "mbias") for _ in range(2)]; r_mbias = [R() for _ in range(2)]
                    gh = [sb(at, [128, NJ, 128], BF16, "gh") for _ in range(2)]; r_gh = [R(), R()]
                    pS = [ps(at, [128, 4, 128], F32, "pS") for _ in range(5)]; r_pS = [R() for _ in range(5)]
                    pacc = [ps(at, [128, 512], F32, "pacc") for _ in range(2)]; r_pacc = [R(), R()]
                    pTy = ps(at, [128, 8, 128], BF16, "pTy"); r_pTy = R()
                    nheads = H if stage >= 5 else 1

                    def load_head(hh):
                        b = hh % 2
                        def ld_k(a):
                            sc.dma(SY, "KT%dq%d" % (b, a), KT[b][:, a * 2048:(a + 1) * 2048], kscr[hh, :, a * 2048:(a + 1) * 2048],
                                   reads=r_kchunk[a * 4:(a + 1) * 4], writes=[r_KT[b][a]])

                        def ld_v(a):
                            sc.dma(SY, "Vh%dh%d" % (b, a), Vh[b][:, a * 32:(a + 1) * 32, :], vscr[hh, :, a * 32:(a + 1) * 32, :],
                                   reads=r_vchunk[a * 8:(a + 1) * 8], writes=[r_Vh[b][a]])

                        sc.dma(SY, "qh%d" % b, qh[b][:], qscr[hh], reads=[r_qscr], writes=[r_qh[b]])
                        ld_k(0)
                        ld_v(0)
                        sc.dma(SY, "gh%d" % b, gh[b][:], gscr[hh], reads=[r_gscr], writes=[r_gh[b]])
                        ld_k(1)
                        ld_k(2)
                        ld_k(3)
                        ld_v(1)

                    groups = []
                    for hh in range(nheads):
                        for j in range(NJ):
                            nk = 8 * j + 8
                            for gk in range(nk // 4):
                                groups.append((hh, j, gk, nk))

                    def emit_S(gi):
                        hh, j, gk, nk = groups[gi]
                        b = hh % 2
                        s4 = gi % 5
                        for i in range(4):
                            kt = gk * 4 + i
                            sc.op(P, lambda e, i=i, kt=kt, hh=hh, j=j, b=b, s4=s4: e.matmul(out=pS[s4][:, i, :], lhsT=KT[b][:, kt * 128:(kt + 1) * 128],
                                                                                             rhs=qh[b][:, j * 128:(j + 1) * 128], start=True, stop=True),
                                  reads=[r_KT[b][kt // 16], r_qh[b]], writes=[r_pS[s4]], sig=(i == 3))

                    def emit_soft(gi):
                        hh, j, gk, nk = groups[gi]
                        s4 = gi % 5
                        p4 = gi % 4
                        s3 = gi % 3
                        off = 4 * j * (j + 1)
                        diag = gk >= 2 * j
                        kt0 = gk * 4
                        if (not diag) and gk % 7 == 99:
                            for i in range(4):
                                sc.op(A, lambda e, i=i: e.activation(out=pTb[p4][:, i, :], in_=pS[s4][:, i, :], func=AF.Exp,
                                                                     bias=biasall[:, off + kt0 + i, hh:hh + 1], scale=1.0),
                                      reads=[r_pS[s4], r_biasall], writes=[r_pTb[p4]])
                            return
                        if diag and gi in dslot:
                            mb_, r_mb = mbias[dslot[gi] % 2], r_mbias[dslot[gi] % 2]
                            sc.op(V, lambda e: e.tensor_tensor(out=Sb[s3][:], in0=pS[s4][:], in1=mb_[:], op=ALU.add),
                                  reads=[r_pS[s4], r_mb], writes=[r_Sb[s3]])
                        elif diag:
                            m0 = kt0 - 8 * j
                            sc.op(V, lambda e: e.tensor_tensor(
                                out=Sb[s3][:], in0=pS[s4][:], in1=biasall[:, off + kt0:off + kt0 + 4, hh:hh + 1].to_broadcast([128, 4, 128]), op=ALU.add),
                                reads=[r_pS[s4], r_biasall], writes=[r_Sb[s3]])
                            sc.op(V, lambda e: e.tensor_tensor(out=Sb[s3][:], in0=Sb[s3][:], in1=maskadd[:, m0:m0 + 4, :], op=ALU.add),
                                  reads=[r_Sb[s3], r_mask], writes=[r_Sb[s3]])
                        else:
                            sc.op(V, lambda e: e.tensor_tensor(
                                out=Sb[s3][:], in0=pS[s4][:], in1=biasall[:, off + kt0:off + kt0 + 4, hh:hh + 1].to_broadcast([128, 4, 128]), op=ALU.add),
                                reads=[r_pS[s4], r_biasall], writes=[r_Sb[s3]])
                        sc.op(A, lambda e: e.activation(out=pTb[p4][:], in_=Sb[s3][:], func=AF.Exp), reads=[r_Sb[s3]], writes=[r_pTb[p4]])

                    dslot = {}
                    dlist = []
                    for gi_, (hh_, j_, gk_, nk_) in enumerate(groups):
                        if gk_ >= 2 * j_:
                            dslot[gi_] = len(dlist)
                            dlist.append(gi_)

                    def emit_mb(di_):
                        gi_ = dlist[di_]
                        hh, j, gk, nk = groups[gi_]
                        off = 4 * j * (j + 1)
                        kt0 = gk * 4
                        m0 = kt0 - 8 * j
                        mb_, r_mb = mbias[di_ % 2], r_mbias[di_ % 2]
                        sc.op(G, lambda e: e.tensor_tensor(out=mb_[:], in0=maskadd[:, m0:m0 + 4, :],
                                                           in1=biasall[:, off + kt0:off + kt0 + 4, hh:hh + 1].to_broadcast([128, 4, 128]), op=ALU.add),
                              reads=[r_mask, r_biasall], writes=[r_mb])

                    def emit_PV(gi):
                        hh, j, gk, nk = groups[gi]
                        b = hh % 2
                        p4 = gi % 4
                        pa, r_pa = pacc[(hh * NJ + j) % 2], r_pacc[(hh * NJ + j) % 2]
                        for i in range(4):
                            kt = gk * 4 + i
                            sc.op(P, lambda e, i=i, kt=kt: e.matmul(out=pa[:, 0:129], lhsT=pTb[p4][:, i, :], rhs=Vh[b][:, kt, :],
                                                                   start=(kt == 0), stop=(kt == nk - 1)),
                                  reads=[r_pTb[p4], r_Vh[b][kt // 32]], writes=[r_pa], sig=(i == 3))

                    epi = {"ci": 0}

                    def emit_epi(gi):
                        hh, j, gk, nk = groups[gi]
                        ci = epi["ci"]
                        epi["ci"] += 1
                        pa, r_pa = pacc[(hh * NJ + j) % 2], r_pacc[(hh * NJ + j) % 2]
                        rc = rec[:, (ci % 4):(ci % 4) + 1]
                        yf, r_yf = yfox[ci % 2], r_yfox[ci % 2]
                        sc.op(V, lambda e: e.reciprocal(out=rc, in_=pa[:, 128:129]), reads=[r_pa], writes=[r_rec])
                        sc.op(V, lambda e: e.scalar_tensor_tensor(out=yf[:], in0=pa[:, 0:128], scalar=rc, in1=gh[hh % 2][:, j, :],
                                                                  op0=ALU.mult, op1=ALU.mult),
                              reads=[r_pa, r_rec, r_gh[hh % 2]], writes=[r_yf])
                        sc.op(P, lambda e: e.transpose(out=pTy[:, ci % 8, :], in_=yf[:], identity=ident[:]), reads=[r_yf, r_ident], writes=[r_pTy])
                        sc.op(A, lambda e: e.copy(out=yfT[:, hh, j * 128:(j + 1) * 128], in_=pTy[:, ci % 8, :]), reads=[r_pTy], writes=[r_yfT])

                    def is_last(gi):
                        hh, j, gk, nk = groups[gi]
                        return gk == nk // 4 - 1

                    load_head(0)
                    emit_S(0)
                    emit_S(1)
                    emit_S(2)
                    ng = len(groups)
                    emit_mb(0)
                    for gi in range(ng + 3):
                        if gi in dslot and dslot[gi] + 1 < len(dlist):
                            emit_mb(dslot[gi] + 1)
                        if gi < ng:
                            if gi + 3 < ng:
                                emit_S(gi + 3)
                            emit_soft(gi)
                        if 0 <= gi - 1 < ng:
                            emit_PV(gi - 1)
                        if gi < ng:
                            hh, j, gk, nk = groups[gi]
                            if j == 1 and gk == 1 and hh + 1 < nheads:
                                load_head(hh + 1)
                                if nheads == H and 2 <= hh <= 5:
                                    wq = hh - 2
                                    wdst, rw, key = (woA, r_woA, "woA") if wq < 2 else (woB, r_woB, "woB")
                                    a4 = (wq % 2) * 4
                                    sc.dma(SY, key, wdst[:, a4:a4 + 4, :], wobf[wq * 512:(wq + 1) * 512, :].rearrange("(k p) c -> p k c", p=128),
                                           reads=[r_wobf[wq]], writes=[rw])
                        if 0 <= gi - 3 < ng and is_last(gi - 3):
                            emit_epi(gi - 3)
                    barrier()
                with ExitStack() as op_:
                    fg_rep = sb(op_, [128, D], F32, "fgrep"); r_fg = R()
                    xos = [(sb(op_, [128, D], F32, "xo"), R(), sb(op_, [128, 4], F32, "sso"), R()) for _ in range(2)]
                    obuf = sb(op_, [128, D], F32, "obuf"); r_obuf = R()
                    pmo = [ps(op_, [128, 512], F32, "pmo") for _ in range(4)]; r_pmo = [R() for _ in range(4)]
                    sc.dma(SY, "c4", fg_rep[:], fg_rep_d, writes=[r_fg])

                    def ysrc(mt, j):
                        if mt < 4:
                            return ycT[:, mt, j * 128:(j + 1) * 128], r_ycT
                        if mt < 12:
                            return yfT[:, mt - 4, j * 128:(j + 1) * 128], r_yfT
                        return ymT[:, mt - 12, j * 128:(j + 1) * 128], r_ymT
                    mi = 0
                    for j in range(NJ):
                        xo, r_xo, sso, r_sso = xos[j % 2]
                        sc.dma(SY, "xo%d" % (j % 2), xo[:], x_own[j * 128:(j + 1) * 128, :], writes=[r_xo])
                        for ncg in range(4):
                            pm, r_pm = pmo[mi % 4], r_pmo[mi % 4]
                            for mt in range(16):
                                ya, r_ya = ysrc(mt, j)
                                wsrc, r_ws = (woA, r_woA) if mt < 8 else (woB, r_woB)
                                sc.op(P, lambda e, mt=mt, ncg=ncg, pm=pm, ya=ya, wsrc=wsrc: e.matmul(out=pm[:], lhsT=ya, rhs=wsrc[:, mt % 8, ncg * 512:(ncg + 1) * 512],
                                                                                                     start=(mt == 0), stop=(mt == 15)),
                                      reads=[r_ya, r_ws], writes=[r_pm], sig=(mt == 15))
                            sc.op(V, lambda e, ncg=ncg, pm=pm, xo=xo: e.tensor_tensor(out=xo[:, ncg * 512:(ncg + 1) * 512], in0=pm[:],
                                                                                     in1=xo[:, ncg * 512:(ncg + 1) * 512], op=ALU.add),
                                  reads=[r_pm, r_xo], writes=[r_xo])
                            mi += 1
                        sc.op(A, lambda e, xo=xo, sso=sso: e.activation(out=obuf[:], in_=xo[:], func=AF.Square, accum_out=sso[:, 0:1]),
                              reads=[r_xo], writes=[r_obuf, r_sso])
                        sc.op(A, lambda e, sso=sso: e.activation(out=sso[:, 1:2], in_=sso[:, 0:1], func=AF.Sqrt, bias=cst[:, 0:1], scale=1.0 / D),
                              reads=[r_sso, r_cst], writes=[r_sso])
                        sc.op(V, lambda e, sso=sso: e.reciprocal(out=sso[:, 2:3], in_=sso[:, 1:2]), reads=[r_sso], writes=[r_sso])
                        sc.op(V, lambda e, xo=xo, sso=sso: e.scalar_tensor_tensor(out=obuf[:], in0=xo[:], scalar=sso[:, 2:3], in1=fg_rep[:],
                                                                                  op0=ALU.mult, op1=ALU.mult),
                              reads=[r_xo, r_sso, r_fg], writes=[r_obuf])
                        sc.dma(SY, "obuf", out[j * 128:(j + 1) * 128, :], obuf[:], reads=[r_obuf])
                    barrier()

        toks = sc.final_tokens()
        sc.E[SY].wait(toks)

        with nc.Block() as block:
            @block.sync
            def _(eng):
                sc.replay(SY, eng)

            @block.scalar
            def _(eng):
                sc.replay(A, eng)

            @block.vector
            def _(eng):
                sc.replay(V, eng)

            @block.gpsimd
            def _(eng):
                sc.replay(G, eng)

            @block.tensor
            def _(eng):
                sc.replay(P, eng)
    return nc


def prep_inputs(inputs, c):
    f32 = np.float32
    x = np.asarray(inputs["x"], f32)[0]
    tiles = [c + 8 * j for j in range(NJ)]
    x_own = np.concatenate([x[t * 128:(t + 1) * 128] for t in tiles], axis=0)
    x_halo = np.zeros((256, D), f32)
    for j, t in enumerate(tiles):
        if t > 0:
            x_halo[j * 32:(j + 1) * 32] = x[t * 128 - 32:t * 128]
    rep = lambda v: np.ascontiguousarray(np.broadcast_to(np.asarray(v, f32).reshape(1, -1), (128, np.asarray(v).size)))
    conv_w = np.asarray(inputs["conv_w"], f32)[0]
    conv_w_t = np.ascontiguousarray(conv_w.T.reshape(4, 128, 31).transpose(1, 0, 2))
    par = np.stack([np.asarray(inputs["conv_b"], f32)[0], np.asarray(inputs["conv_ln_g"], f32)[0],
                    np.asarray(inputs["conv_ln_b"], f32)[0]])
    conv_par = np.ascontiguousarray(par.reshape(3, 4, 128).transpose(2, 0, 1))
    k = np.arange(128)
    uinc = (k[:, None] <= k[None, :]).astype(f32)
    m = np.arange(8)
    val = 128 * (m[None, :, None] - c) + k[:, None, None] - k[None, None, :]
    maskadd = np.where(val <= 0, 0.0, NEG).astype(f32)
    tt = np.arange(NT)
    sel = (tt[None, :] < (c + 8 * np.arange(NJ))[:, None]).astype(f32)
    sel = np.ascontiguousarray(np.broadcast_to(sel[None], (128, NJ, NT)))
    return {
        "x_all": x, "x_own": x_own, "x_halo": x_halo, "mem": np.asarray(inputs["mem"], f32)[0],
        "norm_g_rep": rep(inputs["norm_g"]), "mem_norm_g_rep": rep(inputs["mem_norm_g"]),
        "final_g_rep": rep(inputs["final_g"]), "w_in": np.asarray(inputs["w_in"], f32)[0],
        "b_f_rep": rep(inputs["b_f"]), "conv_w_t": conv_w_t, "conv_par": conv_par,
        "w_conv_pw": np.asarray(inputs["w_conv_pw"], f32)[0], "w_mem_kv": np.asarray(inputs["w_mem_kv"], f32)[0],
        "w_out": np.asarray(inputs["w_out"], f32)[0],
        "ident": np.eye(128, dtype=f32).astype(ml_dtypes.bfloat16), "uinc": uinc, "maskadd": maskadd.astype(ml_dtypes.bfloat16), "sel": sel,
    }


def kernel(**inputs):
    nc = build_nc()
    in_maps = [prep_inputs(inputs, c) for c in range(NCORES)]
    res = run_bass_kernel_spmd(nc, in_maps, core_ids=list(range(NCORES)))
    outp = np.zeros((1, S, D), np.float32)
    for c in range(NCORES):
        o = np.asarray(res.results[c]["out"], np.float32)
        for j in range(NJ):
            t = c + 8 * j
            outp[0, t * 128:(t + 1) * 128] = o[j * 128:(j + 1) * 128]
    return outp
```
